# Optimizing a Trainium2 kernel written in Bass

```python
import math
import jax
import jax.numpy as jnp
from jax import lax
import numpy as np

D_MODEL = 4096
BATCH = 1
SEQ = 16384
DEPTH = 4

MIX_WIDTH = D_MODEL
HEAD_DIM = 128
CONV_WIDTH = MIX_WIDTH // 2
CONV_K = 31
DN_WIDTH = MIX_WIDTH // 2
DN_HEAD_DIM = HEAD_DIM
DN_HEADS = DN_WIDTH // DN_HEAD_DIM
DN_CONV_K = 4
DN_CHUNK = 64
EVEN_SPLITS = (CONV_WIDTH, CONV_WIDTH, 3 * DN_WIDTH, DN_WIDTH, DN_HEADS, DN_HEADS)
EVEN_IN = sum(EVEN_SPLITS)
SC_WIDTH = MIX_WIDTH // 2
SC_K = 3
POOL_WIDTH = MIX_WIDTH // 2
POOL_WINDOWS = (2, 4, 8, 16)
POOL_GROUPS = len(POOL_WINDOWS)
POOL_GROUP_DIM = POOL_WIDTH // POOL_GROUPS
ODD_SPLITS = (SC_WIDTH, SC_WIDTH, SC_WIDTH, POOL_WIDTH)
ODD_IN = sum(ODD_SPLITS)
D_FF = 256 * ((8 * D_MODEL // 3 + 255) // 256)
FFN_CONV_K = 3
N_EVEN = (DEPTH + 1) // 2
N_ODD = DEPTH // 2
NORM_EPS = 1e-6

kernel_name = 'hybrid_conformer_deltanet_shortconv_pool_trunk'


def rms_norm(x, g):
    xf = x.astype(jnp.float32)
    y = xf * lax.rsqrt(jnp.mean(xf * xf, axis=-1, keepdims=True) + NORM_EPS)
    return (y * g.astype(jnp.float32)).astype(x.dtype)


def layer_norm(x, g, b):
    xf = x.astype(jnp.float32)
    xc = xf - jnp.mean(xf, axis=-1, keepdims=True)
    y = xc * lax.rsqrt(jnp.mean(xc * xc, axis=-1, keepdims=True) + NORM_EPS)
    return (y * g.astype(jnp.float32) + b.astype(jnp.float32)).astype(x.dtype)


def l2_normalize(x):
    return x * lax.rsqrt(jnp.sum(x * x, axis=-1, keepdims=True) + NORM_EPS)


def causal_dwconv(x, w):
    k, c = w.shape
    return lax.conv_general_dilated(
        x, w[:, None, :].astype(x.dtype), window_strides=(1,), padding=[(k - 1, 0)],
        dimension_numbers=('NWC', 'WIO', 'NWC'), feature_group_count=c)


def split_last(t, sizes):
    return jnp.split(t, [int(s) for s in np.cumsum(sizes)[:-1]], axis=-1)


def chunk_gated_delta_rule(q, k, v, beta, g):
    bsz, seq, heads, dk = q.shape
    dv = v.shape[-1]
    n_chunks = seq // DN_CHUNK

    def to_chunks(t):
        t = jnp.moveaxis(t, 2, 1)
        return t.reshape(t.shape[:2] + (n_chunks, DN_CHUNK) + t.shape[3:])

    q, k, v, beta, g = (to_chunks(t) for t in (q, k, v, beta, g))
    gc = jnp.cumsum(g, axis=-1)
    lower = jnp.tril(jnp.ones((DN_CHUNK, DN_CHUNK), dtype=bool))
    strict = jnp.tril(jnp.ones((DN_CHUNK, DN_CHUNK), dtype=bool), -1)
    diff = gc[..., :, None] - gc[..., None, :]
    decay_mat = jnp.where(lower, jnp.exp(jnp.where(lower, diff, 0.0)), 0.0)
    k_beta = k * beta[..., None]
    m = jnp.where(strict, jnp.einsum('bhncd,bhnsd->bhncs', k_beta, k) * decay_mat, 0.0)
    eye = jnp.eye(DN_CHUNK, dtype=jnp.float32)
    t_inv = lax.linalg.triangular_solve(eye + m, jnp.broadcast_to(eye, m.shape),
                                        left_side=True, lower=True, unit_diagonal=True)
    u = t_inv @ (v * beta[..., None])
    w = t_inv @ (k_beta * jnp.exp(gc)[..., None])
    a_qk = jnp.einsum('bhncd,bhnsd->bhncs', q, k) * decay_mat
    g_last = gc[..., -1]
    q_dec = q * jnp.exp(gc)[..., None]
    k_dec = k * jnp.exp(g_last[..., None] - gc)[..., None]

    def step(state, inp):
        q_i, k_i, u_i, w_i, a_i, gl_i = inp
        v_new = u_i - jnp.einsum('bhcd,bhde->bhce', w_i, state)
        o_i = jnp.einsum('bhcd,bhde->bhce', q_i, state) + jnp.einsum('bhcs,bhse->bhce', a_i, v_new)
        state = state * jnp.exp(gl_i)[..., None, None] + jnp.einsum('bhcd,bhce->bhde', k_i, v_new)
        return state, o_i

    xs = tuple(jnp.moveaxis(t, 2, 0) for t in (q_dec, k_dec, u, w, a_qk, g_last))
    state0 = jnp.zeros((bsz, heads, dk, dv), jnp.float32)
    _, o = lax.scan(step, state0, xs)
    return jnp.transpose(o, (1, 0, 3, 2, 4)).reshape(bsz, seq, heads, dv)


def even_mixer(h, w_in, w_out, a_conv_w, a_conv_b, a_ln_g, a_ln_b,
               dn_conv_w, dn_a_log, dn_dt_bias, dn_norm_g):
    bsz, seq, _ = h.shape
    f32 = jnp.float32
    a_val, a_gate, qkv, z, b_logit, a_logit = split_last(h @ w_in, EVEN_SPLITS)
    u = a_val * jax.nn.sigmoid(a_gate)
    u = causal_dwconv(u, a_conv_w) + a_conv_b.astype(u.dtype)
    u = jax.nn.silu(layer_norm(u, a_ln_g, a_ln_b))
    qkv = jax.nn.silu(causal_dwconv(qkv, dn_conv_w)).astype(f32)
    q, k, v = [t.reshape(bsz, seq, DN_HEADS, DN_HEAD_DIM) for t in jnp.split(qkv, 3, axis=-1)]
    q = l2_normalize(q) * (DN_HEAD_DIM ** -0.5)
    k = l2_normalize(k)
    beta = jax.nn.sigmoid(b_logit.astype(f32))
    g = -jnp.exp(dn_a_log.astype(f32)) * jax.nn.softplus(a_logit.astype(f32) + dn_dt_bias.astype(f32))
    o = chunk_gated_delta_rule(q, k, v, beta, g)
    zf = z.astype(f32).reshape(bsz, seq, DN_HEADS, DN_HEAD_DIM)
    o = (rms_norm(o, dn_norm_g) * jax.nn.silu(zf)).reshape(bsz, seq, DN_WIDTH).astype(h.dtype)
    return jnp.concatenate([u, o], axis=-1) @ w_out


def multiscale_pool(p, pool_w, pool_scale):
    bsz, seq, _ = p.shape
    pf = p.astype(jnp.float32)
    cs = jnp.cumsum(pf, axis=1)
    t_count = jnp.arange(1, seq + 1, dtype=jnp.float32)[:, None]
    groups = []
    for gi, win in enumerate(POOL_WINDOWS):
        sl = slice(gi * POOL_GROUP_DIM, (gi + 1) * POOL_GROUP_DIM)
        cs_g = cs[..., sl]
        lagged = jnp.pad(cs_g[:, :seq - win], ((0, 0), (win, 0), (0, 0)))
        groups.append((cs_g - lagged) / jnp.minimum(t_count, float(win)) - pf[..., sl])
    mixed = jnp.stack(groups, axis=2)
    mixed = jnp.einsum('btgc,gcd->btgd', mixed, pool_w.astype(jnp.float32))
    mixed = mixed * pool_scale.astype(jnp.float32).reshape(POOL_GROUPS, POOL_GROUP_DIM)
    return mixed.reshape(bsz, seq, POOL_WIDTH).astype(p.dtype)


def odd_mixer(h, w_in, w_out, sc_conv_w, pool_w, pool_scale):
    gate_b, gate_c, val, p = split_last(h @ w_in, ODD_SPLITS)
    c_out = gate_b * causal_dwconv(gate_c * val, sc_conv_w)
    d_out = multiscale_pool(p, pool_w, pool_scale)
    return jnp.concatenate([c_out, d_out], axis=-1) @ w_out


def conv_ffn(h, w_gate, conv_w, w_up, w_down):
    gate = causal_dwconv(h @ w_gate, conv_w)
    return (jax.nn.silu(gate) * (h @ w_up)) @ w_down


def setup_inputs(seed: int = 0) -> dict:
    key = jax.random.key(seed)
    ks = jax.random.split(key, 24)
    f32 = jnp.float32

    def nrm(k, shape, scale):
        return jax.random.normal(k, shape, f32) * scale

    def gain(k, shape):
        return 1.0 + nrm(k, shape, 0.02)

    dt = jnp.exp(jax.random.uniform(ks[12], (N_EVEN, DN_HEADS), f32)
                 * (math.log(0.1) - math.log(0.001)) + math.log(0.001))
    return {
        'x': nrm(ks[0], (BATCH, SEQ, D_MODEL), 1.0),
        'mix_norm_g': gain(ks[1], (DEPTH, D_MODEL)),
        'ffn_norm_g': gain(ks[2], (DEPTH, D_MODEL)),
        'final_norm_g': gain(ks[3], (D_MODEL,)),
        'ev_w_in': nrm(ks[4], (N_EVEN, D_MODEL, EVEN_IN), D_MODEL ** -0.5),
        'ev_w_out': nrm(ks[5], (N_EVEN, MIX_WIDTH, D_MODEL), MIX_WIDTH ** -0.5),
        'a_conv_w': nrm(ks[6], (N_EVEN, CONV_K, CONV_WIDTH), CONV_K ** -0.5),
        'a_conv_b': nrm(ks[7], (N_EVEN, CONV_WIDTH), 0.02),
        'a_ln_g': gain(ks[8], (N_EVEN, CONV_WIDTH)),
        'a_ln_b': nrm(ks[9], (N_EVEN, CONV_WIDTH), 0.02),
        'dn_conv_w': nrm(ks[10], (N_EVEN, DN_CONV_K, 3 * DN_WIDTH), DN_CONV_K ** -0.5),
        'dn_a_log': jnp.log(jax.random.uniform(ks[11], (N_EVEN, DN_HEADS), f32, 1.0, 16.0)),
        'dn_dt_bias': dt + jnp.log(-jnp.expm1(-dt)),
        'dn_norm_g': gain(ks[13], (N_EVEN, DN_HEAD_DIM)),
        'od_w_in': nrm(ks[14], (N_ODD, D_MODEL, ODD_IN), D_MODEL ** -0.5),
        'od_w_out': nrm(ks[15], (N_ODD, MIX_WIDTH, D_MODEL), MIX_WIDTH ** -0.5),
        'sc_conv_w': nrm(ks[16], (N_ODD, SC_K, SC_WIDTH), SC_K ** -0.5),
        'pool_w': nrm(ks[17], (N_ODD, POOL_GROUPS, POOL_GROUP_DIM, POOL_GROUP_DIM), POOL_GROUP_DIM ** -0.5),
        'pool_scale': gain(ks[18], (N_ODD, POOL_WIDTH)),
        'ffn_w_gate': nrm(ks[19], (DEPTH, D_MODEL, D_FF), D_MODEL ** -0.5),
        'ffn_conv_w': nrm(ks[20], (DEPTH, FFN_CONV_K, D_FF), FFN_CONV_K ** -0.5),
        'ffn_w_up': nrm(ks[21], (DEPTH, D_MODEL, D_FF), D_MODEL ** -0.5),
        'ffn_w_down': nrm(ks[22], (DEPTH, D_FF, D_MODEL), D_FF ** -0.5),
    }


def reference(x, mix_norm_g, ffn_norm_g, final_norm_g,
              ev_w_in, ev_w_out, a_conv_w, a_conv_b, a_ln_g, a_ln_b,
              dn_conv_w, dn_a_log, dn_dt_bias, dn_norm_g,
              od_w_in, od_w_out, sc_conv_w, pool_w, pool_scale,
              ffn_w_gate, ffn_conv_w, ffn_w_up, ffn_w_down):
    for layer in range(DEPTH):
        j = layer // 2
        h = rms_norm(x, mix_norm_g[layer])
        if layer % 2 == 0:
            x = x + even_mixer(h, ev_w_in[j], ev_w_out[j], a_conv_w[j], a_conv_b[j],
                               a_ln_g[j], a_ln_b[j], dn_conv_w[j], dn_a_log[j],
                               dn_dt_bias[j], dn_norm_g[j])
        else:
            x = x + odd_mixer(h, od_w_in[j], od_w_out[j], sc_conv_w[j], pool_w[j], pool_scale[j])
        h = rms_norm(x, ffn_norm_g[layer])
        x = x + conv_ffn(h, ffn_w_gate[layer], ffn_conv_w[layer], ffn_w_up[layer], ffn_w_down[layer])
    return rms_norm(x, final_norm_g)
```

```python
import numpy as np
from contextlib import ExitStack
import concourse.bass as bass
import concourse.mybir as mybir
from concourse.bass_utils import run_bass_kernel_spmd

F32 = mybir.dt.float32
BF16 = mybir.dt.bfloat16
AF = mybir.ActivationFunctionType
ALU = mybir.AluOpType
NDS = 8
DN_DEBUG = 0
EPS = 1e-6
BIG = 1.0e4


class Cfg:
    def __init__(self, D=4096, SEQ=16384, NC=8, TT=512, depth=4):
        self.D = D
        self.SEQ = SEQ
        self.NC = NC
        self.NT = SEQ // NC
        self.TT = min(TT, self.NT)
        self.H = 32
        self.depth = depth
        self.KC = D // 128
        self.CW = D // 2
        self.DW = D // 2
        self.NH = self.DW // 128
        self.CK = 31
        self.DNK = 4
        self.FF = 256 * ((8 * D // 3 + 255) // 256)
        self.FC = self.FF // 128
        self.PG = (D // 2) // 4
        self.EVEN_IN = 2 * self.CW + 4 * self.DW + 2 * self.NH
        self.ODD_IN = 4 * (D // 2)
        self.NTT = self.NT // self.TT
        self.NCH = self.NT // 64
        self.TD = 256 if self.NT >= 512 else self.NT // 2
        self.CPD = self.TD // 64
        self.NTD = self.NT // self.TD
        self.tiles = [(0, self.H)] + [(self.H + i * self.TT, self.TT) for i in range(self.NTT)]


_UID = [0]


def sbuf_t(nc, name, shape, dt):
    _UID[0] += 1
    return nc.sbuf_tensor("%s_%d" % (name, _UID[0]), shape, dt)


class TR:
    __slots__ = ("w", "r")

    def __init__(self):
        self.w = None
        self.r = {}


def trs(n):
    return [TR() for _ in range(n)]


class KB:
    def __init__(self, nc, es):
        self.nc = nc
        self.eng = {"pe": nc.tensor, "dve": nc.vector, "act": nc.scalar, "pool": nc.gpsimd, "sp": nc.sync}
        self.sems = {}
        self.ecnt = {}
        for e in ("pe", "dve", "act", "pool"):
            self.sems[e] = es.enter_context(nc.semaphore("s_" + e))
            self.ecnt[e] = 0
        self.waited = {e: {} for e in self.eng}
        self.dsem = {}
        self.drr = {}
        for q in ("sp", "pool", "act"):
            lst = []
            for i in range(NDS):
                key = "d_%s%d" % (q, i)
                self.sems[key] = es.enter_context(nc.semaphore(key))
                lst.append([key, 0])
            self.dsem[q] = lst
            self.drr[q] = 0
        self.n_ins = 0
        self.n_wait = 0

    def _wait(self, e, deps):
        wd = self.waited[e]
        for key, val in deps:
            if wd.get(key, 0) >= val:
                continue
            self.eng[e].wait_ge(self.sems[key], val)
            wd[key] = val
            self.n_wait += 1

    def _deps(self, e, rd, wr, is_dma):
        deps = set()
        for t in rd:
            if t.w is not None:
                if is_dma or t.w[0] != e or e != "pe":
                    deps.add(t.w)
        for t in wr:
            if t.w is not None and (is_dma or t.w[0] != e):
                deps.add(t.w)
            for k, v in t.r.items():
                if is_dma or k != e:
                    deps.add((k, v))
        return deps

    def ins(self, e, fn, rd=(), wr=()):
        self._wait(e, self._deps(e, rd, wr, False))
        inst = fn()
        self.ecnt[e] += 1
        idx = self.ecnt[e]
        inst.then_inc(self.sems[e], 1)
        for t in wr:
            t.w = (e, idx)
            t.r = {}
        for t in rd:
            t.r[e] = idx
        self.n_ins += 1
        return inst

    def dma(self, q, out, in_, rd=(), wr=()):
        deps = self._deps(q, rd, wr, True)
        lst = self.dsem[q]
        i = self.drr[q]
        self.drr[q] = (i + 1) % NDS
        key, cnt = lst[i]
        if cnt > 0:
            deps.add((key, 16 * cnt))
        self._wait(q, deps)
        lst[i][1] = cnt + 1
        self.eng[q].dma_start(out=out, in_=in_).then_inc(self.sems[key], 16)
        tok = (key, 16 * (cnt + 1))
        for t in wr:
            t.w = tok
            t.r = {}
        for t in rd:
            if t.r.get(key, 0) < tok[1]:
                t.r[key] = tok[1]
        self.n_ins += 1

    def barrier(self):
        allv = []
        for e in ("pe", "dve", "act", "pool"):
            if self.ecnt[e] > 0:
                allv.append((e, self.ecnt[e]))
        for q in self.dsem:
            for key, cnt in self.dsem[q]:
                if cnt > 0:
                    allv.append((key, 16 * cnt))
        for e in self.eng:
            self._wait(e, allv)


class PS:
    def __init__(self, nc, es):
        self.banks = [es.enter_context(nc.psum_tensor("psb%d" % i, [128, 512], F32)) for i in range(8)]
        self.tr = trs(8)
        self.groups = {}
        self.group("mm", [0, 1, 2, 3, 4])
        self.group("aux", [5, 6, 7])

    def group(self, name, idxs):
        self.groups[name] = [list(idxs), 0]

    def next(self, name):
        g = self.groups[name]
        i = g[0][g[1] % len(g[0])]
        g[1] += 1
        return self.banks[i], self.tr[i]


def const_layout(cfg):
    lay = {}
    off = 0
    for name, w in (("ident", 128), ("tri", 64), ("sel63", 128), ("pm1", 64), ("pm2", 64), ("ns1", 64), ("ns2", 64)):
        lay[name] = (off, w)
        off += w
    return lay, off


def const_array(cfg):
    lay, tot = const_layout(cfg)
    c = np.zeros((128, tot), np.float32)

    def put(name, arr):
        o, w = lay[name]
        c[:arr.shape[0], o:o + w] = arr
    put("ident", np.eye(128, dtype=np.float32))
    s = np.arange(64)
    put("tri", (s[:, None] <= s[None, :]).astype(np.float32))
    sel = np.zeros((64, 128), np.float32)
    sel[63, :] = 1.0
    put("sel63", sel)
    put("pm1", BIG * (s[None, :] > s[:, None]).astype(np.float32))
    put("pm2", -BIG * (s[None, :] < s[:, None]).astype(np.float32))
    put("ns1", -(s[None, :] < s[:, None]).astype(np.float32))
    put("ns2", -(s[:, None] < s[None, :]).astype(np.float32))
    return c


def const2_array(cfg):
    eh = np.zeros((cfg.NH, cfg.NH, 128), np.float32)
    for h in range(cfg.NH):
        eh[h, h, :] = 1.0
    rm = np.ones((cfg.NH, cfg.NT), np.float32)
    rm[:, ::64] = 0.0
    return np.concatenate([eh.reshape(cfg.NH, cfg.NH * 128), rm], axis=1)


class Const:
    def __init__(self, kb, nc, es, cfg, cst_dram):
        lay, tot = const_layout(cfg)
        self.lay = lay
        self.all = es.enter_context(sbuf_t(nc, "cst_all", [128, tot], F32))
        self.ident_b = es.enter_context(sbuf_t(nc, "ident_b", [128, 128], BF16))
        self.ones_b = es.enter_context(sbuf_t(nc, "ones_b", [128, 128], BF16))
        self.ones_f = es.enter_context(sbuf_t(nc, "ones_f", [128, 128], F32))
        self.eps = es.enter_context(sbuf_t(nc, "eps_c", [128, 1], F32))
        self.one = es.enter_context(sbuf_t(nc, "one_c", [128, 1], F32))
        self.tr = TR()
        kb.dma("sp", self.all[:], cst_dram, wr=[self.tr])
        kb.ins("dve", lambda: nc.vector.memset(self.ones_f[:], 1.0), wr=[self.tr])
        kb.ins("dve", lambda: nc.vector.memset(self.ones_b[:], 1.0), wr=[self.tr])
        kb.ins("dve", lambda: nc.vector.memset(self.eps[:], EPS), wr=[self.tr])
        kb.ins("dve", lambda: nc.vector.memset(self.one[:], 1.0), wr=[self.tr])
        kb.ins("dve", lambda: nc.vector.tensor_copy(out=self.ident_b[:], in_=self.c("ident")), rd=[self.tr], wr=[self.tr])
        kb.barrier()

    def c(self, name, parts=128):
        o, w = self.lay[name]
        return self.all[0:parts, o:o + w]


def split2k(ap2d, n):
    if n <= 2048:
        return ap2d
    assert n % 2048 == 0
    return ap2d.rearrange("p (a n) -> p a n", n=2048)


def emit_norm(kb, nc, es, cfg, ps, cst, xT, xh, gv, hT=None, hT_tr=None, out_dram=None):
    KC, H, NT = cfg.KC, cfg.H, cfg.NT
    xs = [es.enter_context(sbuf_t(nc, "nx%d" % i, [128, KC, 128], F32)) for i in range(2)]
    sq = [es.enter_context(sbuf_t(nc, "nsq%d" % i, [128, KC, 128], BF16)) for i in range(2)]
    rs = [es.enter_context(sbuf_t(nc, "nrs%d" % i, [128, 128], F32)) for i in range(2)]
    xs_tr, sq_tr, rs_tr = trs(2), trs(2), trs(2)
    if out_dram is not None:
        ost = [es.enter_context(sbuf_t(nc, "nos%d" % i, [128, KC, 128], F32)) for i in range(2)]
        ost_tr = trs(2)
    subs = []
    if xh is not None:
        subs.append((None, 0, H, 0))
    for i in range(NT // 128):
        subs.append((i * 128, H + i * 128, 128, 1 + i))
    for si, (src0, dst0, w, tri) in enumerate(subs):
        b = si % 2
        if src0 is None:
            src = xh.rearrange("(kc p) t -> p kc t", p=128)
        else:
            src = xT[:, src0:src0 + w].rearrange("(kc p) t -> p kc t", p=128)
        kb.dma("sp", xs[b][:, :, 0:w], src, wr=[xs_tr[b]])
        kb.ins("act", lambda: nc.scalar.activation(out=sq[b][:, :, 0:w], in_=xs[b][:, :, 0:w], func=AF.Square),
               rd=[xs_tr[b]], wr=[sq_tr[b]])
        pt, ptr = ps.next("aux")
        for kc in range(KC):
            kb.ins("pe", lambda: nc.tensor.matmul(pt[:, 0:w], lhsT=cst.ones_b[:], rhs=sq[b][:, kc, 0:w],
                                                  start=(kc == 0), stop=(kc == KC - 1)),
                   rd=[sq_tr[b], cst.tr], wr=[ptr])
        kb.ins("act", lambda: nc.scalar.activation(out=rs[b][:, 0:w], in_=pt[:, 0:w], func=AF.Ln,
                                                   scale=1.0 / cfg.D, bias=cst.eps[:]),
               rd=[ptr, cst.tr], wr=[rs_tr[b]])
        kb.ins("act", lambda: nc.scalar.activation(out=rs[b][:, 0:w], in_=rs[b][:, 0:w], func=AF.Exp, scale=-0.5),
               rd=[rs_tr[b]], wr=[rs_tr[b]])
        for kc in range(KC):
            if out_dram is None:
                o_ap, o_tr = hT[:, kc, dst0:dst0 + w], hT_tr[tri]
            else:
                o_ap, o_tr = ost[b][:, kc, 0:w], ost_tr[b]
            kb.ins("dve", lambda: nc.vector.scalar_tensor_tensor(
                out=o_ap, in0=xs[b][:, kc, 0:w], scalar=gv[:, kc:kc + 1],
                in1=rs[b][:, 0:w], op0=ALU.mult, op1=ALU.mult),
                rd=[xs_tr[b], rs_tr[b]], wr=[o_tr])
        if out_dram is not None:
            kb.dma("sp", out_dram[:, src0:src0 + w].rearrange("(kc p) t -> p kc t", p=128), ost[b][:, :, 0:w],
                   rd=[ost_tr[b]])


def hT_trs_for(cfg, hT_tr, start, width):
    out = []
    if start < cfg.H:
        out.append(hT_tr[0])
    a = max(start, cfg.H) - cfg.H
    b = start + width - cfg.H
    for i in range(a // 128, (b + 127) // 128):
        out.append(hT_tr[1 + i])
    return out


def load_small(kb, nc, es, name, shape, src, q="sp"):
    t = es.enter_context(sbuf_t(nc, name, shape, F32))
    tr = TR()
    kb.dma(q, t[:], src, wr=[tr])
    return t, tr


def build_diag(kb, nc, cst, dg, dg_tr, cw, cw_tr, ntap, ci):
    for tap in range(ntap):
        kb.ins("dve", lambda: nc.vector.tensor_scalar(out=dg[:, tap, :], in0=cst.c("ident"),
                                                      scalar1=cw[:, tap, ci:ci + 1], scalar2=None, op0=ALU.mult),
               rd=[cst.tr, cw_tr], wr=[dg_tr])


def phase_ffn1(kb, nc, cfg, ps, cst, A):
    KC, H, NT, TT, FC, NTT = cfg.KC, cfg.H, cfg.NT, cfg.TT, cfg.FC, cfg.NTT
    with ExitStack() as es:
        hT = es.enter_context(sbuf_t(nc, "hT", [128, KC, H + NT], BF16))
        hT_tr = trs(1 + NT // 128)
        gv, gv_tr = load_small(kb, nc, es, "gv", [128, KC], A["fng"])
        cw, cw_tr = load_small(kb, nc, es, "cw", [128, 3, FC], A["fcw"])
        with ExitStack() as es2:
            emit_norm(kb, nc, es2, cfg, ps, cst, A["xT"], A["xh"], gv, hT=hT, hT_tr=hT_tr)
            kb.barrier()
        NWB = 4
        wb = [es.enter_context(sbuf_t(nc, "wb%d" % i, [128, KC * 128], BF16)) for i in range(NWB)]
        wb_tr = trs(NWB)
        gb = [es.enter_context(sbuf_t(nc, "gb%d" % i, [128, H + NT], BF16)) for i in range(2)]
        gb_tr = [trs(1 + NTT) for _ in range(2)]
        dg = [es.enter_context(sbuf_t(nc, "dg%d" % i, [128, 3, 128], BF16)) for i in range(2)]
        dg_tr = trs(2)
        st = [es.enter_context(sbuf_t(nc, "st%d" % i, [128, TT], F32)) for i in range(2)]
        st_tr = trs(2)
        ob = [es.enter_context(sbuf_t(nc, "ob%d" % i, [128, NT], BF16)) for i in range(2)]
        ob_tr = trs(2)
        blocks = []
        for i in range(FC):
            blocks.append(("wg", i))
            blocks.append(("wu", i))

        def load_w(bi):
            if bi < len(blocks):
                nm, i = blocks[bi]
                kb.dma("pool", split2k(wb[bi % NWB][:], KC * 128), split2k(A[nm][i], KC * 128), wr=[wb_tr[bi % NWB]])
        load_w(0)
        load_w(1)
        sti = 0
        for i in range(FC):
            load_w(2 * i + 2)
            load_w(2 * i + 3)
            p = i % 2
            wgi, wui = (2 * i) % NWB, (2 * i + 1) % NWB
            build_diag(kb, nc, cst, dg[p], dg_tr[p], cw, cw_tr, 3, i)
            for j, (s0, w) in enumerate(cfg.tiles):
                pt, ptr = ps.next("mm")
                for kc in range(KC):
                    kb.ins("pe", lambda: nc.tensor.matmul(pt[:, 0:w], lhsT=wb[wgi][:, kc * 128:(kc + 1) * 128],
                                                          rhs=hT[:, kc, s0:s0 + w], start=(kc == 0), stop=(kc == KC - 1)),
                           rd=[wb_tr[wgi]] + hT_trs_for(cfg, hT_tr, s0, w), wr=[ptr])
                kb.ins("act", lambda: nc.scalar.copy(out=gb[p][:, s0:s0 + w], in_=pt[:, 0:w]),
                       rd=[ptr], wr=[gb_tr[p][j]])
            for j in range(NTT):
                s0 = H + j * TT
                pc, pctr = ps.next("aux")
                for tap in range(3):
                    kb.ins("pe", lambda: nc.tensor.matmul(pc[:, 0:TT], lhsT=dg[p][:, tap, :],
                                                          rhs=gb[p][:, s0 - 2 + tap:s0 - 2 + tap + TT],
                                                          start=(tap == 0), stop=(tap == 2)),
                           rd=[dg_tr[p], gb_tr[p][j], gb_tr[p][j + 1]], wr=[pctr])
                pu, putr = ps.next("mm")
                for kc in range(KC):
                    kb.ins("pe", lambda: nc.tensor.matmul(pu[:, 0:TT], lhsT=wb[wui][:, kc * 128:(kc + 1) * 128],
                                                          rhs=hT[:, kc, s0:s0 + TT], start=(kc == 0), stop=(kc == KC - 1)),
                           rd=[wb_tr[wui]] + hT_trs_for(cfg, hT_tr, s0, TT), wr=[putr])
                sb_ = sti % 2
                sti += 1
                kb.ins("act", lambda: nc.scalar.activation(out=st[sb_][:, :], in_=pc[:, 0:TT], func=AF.Silu),
                       rd=[pctr], wr=[st_tr[sb_]])
                kb.ins("dve", lambda: nc.vector.tensor_tensor(out=ob[p][:, j * TT:(j + 1) * TT], in0=pu[:, 0:TT],
                                                              in1=st[sb_][:, :], op=ALU.mult),
                       rd=[putr, st_tr[sb_]], wr=[ob_tr[p]])
            kb.dma("sp", A["act"][i * 128:(i + 1) * 128, :], ob[p][:, :], rd=[ob_tr[p]])
        kb.barrier()


def phase_ffn2(kb, nc, cfg, ps, cst, A):
    KC, NT, TT, FC, NTT = cfg.KC, cfg.NT, cfg.TT, cfg.FC, cfg.NTT
    with ExitStack() as es:
        actb = es.enter_context(sbuf_t(nc, "actb", [128, FC, TT], BF16))
        G = 8
        ngr = (FC + G - 1) // G
        act_tr = trs(ngr)
        NWB = 3
        wd = [es.enter_context(sbuf_t(nc, "wd%d" % i, [128, FC * 128], BF16)) for i in range(NWB)]
        wd_tr = trs(NWB)
        xs = [es.enter_context(sbuf_t(nc, "xs%d" % i, [128, TT], F32)) for i in range(3)]
        xs_tr = trs(3)
        xo = [es.enter_context(sbuf_t(nc, "xo%d" % i, [128, TT], F32)) for i in range(3)]
        xo_tr = trs(3)
        seq = [(j, cb) for j in range(NTT) for cb in range(KC)]

        def load_w(k):
            if k < len(seq):
                cb = seq[k][1]
                n = FC * 128
                dst, src = wd[k % NWB], A["wd"][cb]
                if n % 2048 == 0:
                    kb.dma("pool", split2k(dst[:], n), split2k(src, n), wr=[wd_tr[k % NWB]])
                else:
                    for a in range(0, n, 2048):
                        b_ = min(n, a + 2048)
                        kb.dma("pool", dst[:, a:b_], src[:, a:b_], wr=[wd_tr[k % NWB]])
        load_w(0)
        load_w(1)
        for k, (j, cb) in enumerate(seq):
            if cb == 0:
                for g in range(ngr):
                    g0, g1 = g * G, min(FC, (g + 1) * G)
                    kb.dma("sp", actb[:, g0:g1, :],
                           A["act"][g0 * 128:g1 * 128, j * TT:(j + 1) * TT].rearrange("(i p) t -> p i t", p=128),
                           wr=[act_tr[g]])
            load_w(k + 2)
            b = k % 3
            kb.dma("sp", xs[b][:, :], A["xT"][cb * 128:(cb + 1) * 128, j * TT:(j + 1) * TT], wr=[xs_tr[b]])
            pt, ptr = ps.next("mm")
            for i in range(FC):
                kb.ins("pe", lambda: nc.tensor.matmul(pt[:, 0:TT], lhsT=wd[k % NWB][:, i * 128:(i + 1) * 128],
                                                      rhs=actb[:, i, :], start=(i == 0), stop=(i == FC - 1)),
                       rd=[wd_tr[k % NWB], act_tr[i // G]], wr=[ptr])
            kb.ins("dve", lambda: nc.vector.tensor_tensor(out=xo[b][:, :], in0=pt[:, 0:TT], in1=xs[b][:, :], op=ALU.add),
                   rd=[ptr, xs_tr[b]], wr=[xo_tr[b]])
            kb.dma("sp", A["xo"][cb * 128:(cb + 1) * 128, j * TT:(j + 1) * TT], xo[b][:, :], rd=[xo_tr[b]])
        kb.barrier()


def phase_final(kb, nc, cfg, ps, cst, A):
    with ExitStack() as es:
        gv, gv_tr = load_small(kb, nc, es, "gvf", [128, cfg.KC], A["fing"])
        kb.barrier()
        emit_norm(kb, nc, es, cfg, ps, cst, A["xT"], None, gv, out_dram=A["out"])
        kb.barrier()


def wload(kb, dst, dst_tr, src, n):
    kb.dma("pool", split2k(dst[:, 0:n], n), split2k(src, n), wr=[dst_tr])


def mm_block(kb, nc, cfg, pt, ptr, w_sb, w_tr, hT, hT_tr, s0, w):
    KC = cfg.KC
    rd = [w_tr] + hT_trs_for(cfg, hT_tr, s0, w)
    for kc in range(KC):
        kb.ins("pe", lambda: nc.tensor.matmul(pt[:, 0:w], lhsT=w_sb[:, kc * 128:(kc + 1) * 128],
                                              rhs=hT[:, kc, s0:s0 + w], start=(kc == 0), stop=(kc == KC - 1)),
               rd=rd, wr=[ptr])


def phase_odd_in(kb, nc, cfg, ps, cst, A):
    KC, H, NT, TT, NTT = cfg.KC, cfg.H, cfg.NT, cfg.TT, cfg.NTT
    NCc = cfg.CW // 128
    CPG = cfg.PG // 128
    WN = KC * 128
    with ExitStack() as es:
        hT = es.enter_context(sbuf_t(nc, "hT", [128, KC, H + NT], BF16))
        hT_tr = trs(1 + NT // 128)
        gv, gv_tr = load_small(kb, nc, es, "gv", [128, KC], A["mng"])
        scw, scw_tr = load_small(kb, nc, es, "scw", [128, 3, NCc], A["scw"])
        psc, psc_tr = load_small(kb, nc, es, "psc", [128, NCc], A["psc"])
        pcorr, pcorr_tr = load_small(kb, nc, es, "pcorr", [128, 4, 16], A["pcorr"])
        with ExitStack() as es2:
            emit_norm(kb, nc, es2, cfg, ps, cst, A["xT"], A["xh"], gv, hT=hT, hT_tr=hT_tr)
            kb.barrier()
        with ExitStack() as e2:
            NWB = 5
            wb = [e2.enter_context(sbuf_t(nc, "wb%d" % i, [128, WN], BF16)) for i in range(NWB)]
            wb_tr = trs(NWB)
            cvb = [e2.enter_context(sbuf_t(nc, "cvb%d" % i, [128, H + NT], BF16)) for i in range(2)]
            cvb_tr = [trs(1 + NTT) for _ in range(2)]
            dg = [e2.enter_context(sbuf_t(nc, "dg%d" % i, [128, 3, 128], BF16)) for i in range(2)]
            dg_tr = trs(2)
            gct = [e2.enter_context(sbuf_t(nc, "gct%d" % i, [128, TT], F32)) for i in range(2)]
            gct_tr = trs(2)
            gbt = [e2.enter_context(sbuf_t(nc, "gbt%d" % i, [128, TT], F32)) for i in range(2)]
            gbt_tr = trs(2)
            ob = [e2.enter_context(sbuf_t(nc, "ob%d" % i, [128, NT], BF16)) for i in range(2)]
            ob_tr = trs(2)
            blocks = []
            for i in range(NCc):
                blocks += [NCc + i, 2 * NCc + i, i]

            def load_w(bi):
                if bi < len(blocks):
                    wload(kb, wb[bi % NWB], wb_tr[bi % NWB], A["wi"][blocks[bi]], WN)
            for bi in range(3):
                load_w(bi)
            tc = 0
            for i in range(NCc):
                load_w(3 * i + 3)
                load_w(3 * i + 4)
                if i == NCc - 1:
                    pass
                p = i % 2
                wc, wv, wg_ = (3 * i) % NWB, (3 * i + 1) % NWB, (3 * i + 2) % NWB
                build_diag(kb, nc, cst, dg[p], dg_tr[p], scw, scw_tr, 3, i)
                for j, (s0, w) in enumerate(cfg.tiles):
                    k = tc % 2
                    tc += 1
                    pa, patr = ps.next("mm")
                    mm_block(kb, nc, cfg, pa, patr, wb[wc], wb_tr[wc], hT, hT_tr, s0, w)
                    kb.ins("act", lambda: nc.scalar.copy(out=gct[k][:, 0:w], in_=pa[:, 0:w]), rd=[patr], wr=[gct_tr[k]])
                    pb_, pbtr = ps.next("mm")
                    mm_block(kb, nc, cfg, pb_, pbtr, wb[wv], wb_tr[wv], hT, hT_tr, s0, w)
                    kb.ins("dve", lambda: nc.vector.tensor_tensor(out=cvb[p][:, s0:s0 + w], in0=pb_[:, 0:w],
                                                                  in1=gct[k][:, 0:w], op=ALU.mult),
                           rd=[pbtr, gct_tr[k]], wr=[cvb_tr[p][j]])
                if i + 1 < NCc:
                    load_w(3 * i + 5)
                for jj in range(NTT):
                    s0 = H + jj * TT
                    k = tc % 2
                    tc += 1
                    pg, pgtr = ps.next("mm")
                    mm_block(kb, nc, cfg, pg, pgtr, wb[wg_], wb_tr[wg_], hT, hT_tr, s0, TT)
                    kb.ins("act", lambda: nc.scalar.copy(out=gbt[k][:, :], in_=pg[:, 0:TT]), rd=[pgtr], wr=[gbt_tr[k]])
                    pc, pctr = ps.next("aux")
                    for tap in range(3):
                        kb.ins("pe", lambda: nc.tensor.matmul(pc[:, 0:TT], lhsT=dg[p][:, tap, :],
                                                              rhs=cvb[p][:, s0 - 2 + tap:s0 - 2 + tap + TT],
                                                              start=(tap == 0), stop=(tap == 2)),
                               rd=[dg_tr[p], cvb_tr[p][jj], cvb_tr[p][jj + 1]], wr=[pctr])
                    kb.ins("dve", lambda: nc.vector.tensor_tensor(out=ob[p][:, jj * TT:(jj + 1) * TT], in0=pc[:, 0:TT],
                                                                  in1=gbt[k][:, :], op=ALU.mult),
                           rd=[pctr, gbt_tr[k]], wr=[ob_tr[p]])
                kb.dma("sp", A["hc"][i * 128:(i + 1) * 128, :], ob[p][:, :], rd=[ob_tr[p]])
            kb.barrier()
        with ExitStack() as e2:
            NWB = 3
            wb = [e2.enter_context(sbuf_t(nc, "wbd%d" % i, [128, WN], BF16)) for i in range(NWB)]
            wb_tr = trs(NWB)
            pbb = [e2.enter_context(sbuf_t(nc, "pbb%d" % i, [128, H + NT], BF16)) for i in range(2)]
            pbb_tr = [trs(1 + NTT) for _ in range(2)]
            pooled = e2.enter_context(sbuf_t(nc, "pooled", [128, CPG, NT], BF16))
            pooled_tr = trs(CPG)
            pwb = [e2.enter_context(sbuf_t(nc, "pwb%d" % i, [128, CPG * cfg.PG], BF16)) for i in range(2)]
            pwb_tr = trs(2)
            dmean = e2.enter_context(sbuf_t(nc, "dmean", [128, 5, 128], BF16))
            dmean_tr = TR()
            tmp16 = e2.enter_context(sbuf_t(nc, "tmp16", [128, 16], F32))
            tmp16_tr = TR()
            ob = [e2.enter_context(sbuf_t(nc, "obd%d" % i, [128, NT], BF16)) for i in range(2)]
            ob_tr = trs(2)
            wins = (2, 4, 8, 16)
            for g in range(4):
                kb.ins("dve", lambda: nc.vector.tensor_scalar(out=dmean[:, g, :], in0=cst.c("ident"), scalar1=1.0 / wins[g],
                                                              scalar2=None, op0=ALU.mult), rd=[cst.tr], wr=[dmean_tr])
            kb.ins("dve", lambda: nc.vector.tensor_scalar(out=dmean[:, 4, :], in0=cst.c("ident"), scalar1=-1.0,
                                                          scalar2=None, op0=ALU.mult), rd=[cst.tr], wr=[dmean_tr])
            nblk = 4 * CPG

            def load_w(bi):
                if bi < nblk:
                    wload(kb, wb[bi % NWB], wb_tr[bi % NWB], A["wi"][3 * NCc + bi], WN)
            load_w(0)
            load_w(1)
            obi = 0
            for g in range(4):
                win = wins[g]
                wload(kb, pwb[g % 2], pwb_tr[g % 2], A["pw"][g], CPG * cfg.PG)
                for kk in range(CPG):
                    bi = g * CPG + kk
                    load_w(bi + 2)
                    q = bi % 2
                    for j, (s0, w) in enumerate(cfg.tiles):
                        pa, patr = ps.next("mm")
                        mm_block(kb, nc, cfg, pa, patr, wb[bi % NWB], wb_tr[bi % NWB], hT, hT_tr, s0, w)
                        kb.ins("act", lambda: nc.scalar.copy(out=pbb[q][:, s0:s0 + w], in_=pa[:, 0:w]),
                               rd=[patr], wr=[pbb_tr[q][j]])
                    for jj in range(NTT):
                        s0 = H + jj * TT
                        pc, pctr = ps.next("aux")
                        for t in range(win):
                            kb.ins("pe", lambda: nc.tensor.matmul(pc[:, 0:TT], lhsT=dmean[:, g, :],
                                                                  rhs=pbb[q][:, s0 - t:s0 - t + TT], start=(t == 0), stop=False),
                                   rd=[dmean_tr, pbb_tr[q][jj], pbb_tr[q][jj + 1]], wr=[pctr])
                        kb.ins("pe", lambda: nc.tensor.matmul(pc[:, 0:TT], lhsT=dmean[:, 4, :],
                                                              rhs=pbb[q][:, s0:s0 + TT], start=False, stop=True),
                               rd=[dmean_tr, pbb_tr[q][jj + 1]], wr=[pctr])
                        kb.ins("act", lambda: nc.scalar.copy(out=pooled[:, kk, jj * TT:(jj + 1) * TT], in_=pc[:, 0:TT]),
                               rd=[pctr], wr=[pooled_tr[kk]])
                    pf, pftr = ps.next("aux")
                    for t in range(win):
                        kb.ins("pe", lambda: nc.tensor.matmul(pf[:, 0:16], lhsT=dmean[:, g, :],
                                                              rhs=pbb[q][:, H - t:H - t + 16], start=(t == 0), stop=(t == win - 1)),
                               rd=[dmean_tr, pbb_tr[q][0], pbb_tr[q][1]], wr=[pftr])
                    kb.ins("dve", lambda: nc.vector.tensor_tensor(out=tmp16[:, :], in0=pf[:, 0:16], in1=pcorr[:, g, :], op=ALU.mult),
                           rd=[pftr, pcorr_tr], wr=[tmp16_tr])
                    kb.ins("dve", lambda: nc.vector.tensor_tensor(out=pooled[:, kk, 0:16], in0=tmp16[:, :], in1=pbb[q][:, H:H + 16],
                                                                  op=ALU.subtract),
                           rd=[tmp16_tr, pbb_tr[q][1]], wr=[pooled_tr[kk]])
                for db in range(CPG):
                    o_ = obi % 2
                    obi += 1
                    for jj in range(NTT):
                        pm, pmtr = ps.next("mm")
                        for kc in range(CPG):
                            kb.ins("pe", lambda: nc.tensor.matmul(
                                pm[:, 0:TT], lhsT=pwb[g % 2][:, kc * cfg.PG + db * 128:kc * cfg.PG + (db + 1) * 128],
                                rhs=pooled[:, kc, jj * TT:(jj + 1) * TT], start=(kc == 0), stop=(kc == CPG - 1)),
                                rd=[pwb_tr[g % 2], pooled_tr[kc]], wr=[pmtr])
                        ch = g * CPG + db
                        kb.ins("act", lambda: nc.scalar.activation(out=ob[o_][:, jj * TT:(jj + 1) * TT], in_=pm[:, 0:TT],
                                                                   func=AF.Identity, scale=psc[:, ch:ch + 1]),
                               rd=[pmtr, psc_tr], wr=[ob_tr[o_]])
                    kb.dma("sp", A["hc"][(NCc + g * CPG + db) * 128:(NCc + g * CPG + db + 1) * 128, :], ob[o_][:, :],
                           rd=[ob_tr[o_]])
            kb.barrier()


def dense_main(kb, nc, es, cfg, ps, hc, hc_tr, wsrc, A):
    KC, NT, TT, NTT = cfg.KC, cfg.NT, cfg.TT, cfg.NTT
    WN = KC * 128
    NWB = 3
    wb = [es.enter_context(sbuf_t(nc, "wbo%d" % i, [128, WN], BF16)) for i in range(NWB)]
    wb_tr = trs(NWB)
    xs = [es.enter_context(sbuf_t(nc, "xs%d" % i, [128, TT], F32)) for i in range(3)]
    xs_tr = trs(3)
    xo = [es.enter_context(sbuf_t(nc, "xo%d" % i, [128, TT], F32)) for i in range(3)]
    xo_tr = trs(3)

    def load_w(cb):
        if cb < KC:
            wload(kb, wb[cb % NWB], wb_tr[cb % NWB], wsrc[cb], WN)
    load_w(0)
    load_w(1)
    k = 0
    for cb in range(KC):
        load_w(cb + 2)
        for jj in range(NTT):
            b = k % 3
            k += 1
            kb.dma("sp", xs[b][:, :], A["xT"][cb * 128:(cb + 1) * 128, jj * TT:(jj + 1) * TT], wr=[xs_tr[b]])
            pt, ptr = ps.next("mm")
            for kc in range(KC):
                kb.ins("pe", lambda: nc.tensor.matmul(pt[:, 0:TT], lhsT=wb[cb % NWB][:, kc * 128:(kc + 1) * 128],
                                                      rhs=hc[:, kc, jj * TT:(jj + 1) * TT], start=(kc == 0), stop=(kc == KC - 1)),
                       rd=[wb_tr[cb % NWB]] + hc_tr[jj], wr=[ptr])
            kb.ins("dve", lambda: nc.vector.tensor_tensor(out=xo[b][:, :], in0=pt[:, 0:TT], in1=xs[b][:, :], op=ALU.add),
                   rd=[ptr, xs_tr[b]], wr=[xo_tr[b]])
            kb.dma("sp", A["xo"][cb * 128:(cb + 1) * 128, jj * TT:(jj + 1) * TT], xo[b][:, :], rd=[xo_tr[b]])


def phase_odd_out(kb, nc, cfg, ps, cst, A):
    KC, NT, TT, NTT = cfg.KC, cfg.NT, cfg.TT, cfg.NTT
    with ExitStack() as es:
        hc = es.enter_context(sbuf_t(nc, "hc", [128, KC, NT], BF16))
        hc_tr = [[TR()] for _ in range(NTT)]
        for jj in range(NTT):
            kb.dma("sp", hc[:, :, jj * TT:(jj + 1) * TT],
                   A["hc"][:, jj * TT:(jj + 1) * TT].rearrange("(kc p) t -> p kc t", p=128), wr=hc_tr[jj])
        dense_main(kb, nc, es, cfg, ps, hc, hc_tr, A["wo"], A)
        kb.barrier()


def phase_even_in(kb, nc, cfg, ps, cst, A):
    KC, H, NT, TT, NTT, NH = cfg.KC, cfg.H, cfg.NT, cfg.TT, cfg.NTT, cfg.NH
    NCc = cfg.CW // 128
    CK = cfg.CK
    WN = KC * 128
    with ExitStack() as es:
        hT = es.enter_context(sbuf_t(nc, "hT", [128, KC, H + NT], BF16))
        hT_tr = trs(1 + NT // 128)
        gv, gv_tr = load_small(kb, nc, es, "gv", [128, KC], A["mng"])
        acw, acw_tr = load_small(kb, nc, es, "acw", [128, CK, NCc], A["acw"])
        acb, acb_tr = load_small(kb, nc, es, "acb", [128, NCc], A["acb"])
        dcw, dcw_tr = load_small(kb, nc, es, "dcw", [128, 4, 3 * NH], A["dcw"])
        alog, alog_tr = load_small(kb, nc, es, "alog", [NH, 1], A["alog"])
        dtb, dtb_tr = load_small(kb, nc, es, "dtb", [NH, 1], A["dtb"])
        with ExitStack() as es2:
            emit_norm(kb, nc, es2, cfg, ps, cst, A["xT"], A["xh"], gv, hT=hT, hT_tr=hT_tr)
            kb.barrier()
        with ExitStack() as e2:
            NWB = 4
            wb = [e2.enter_context(sbuf_t(nc, "wb%d" % i, [128, WN], BF16)) for i in range(NWB)]
            wb_tr = trs(NWB)
            ub = [e2.enter_context(sbuf_t(nc, "ub%d" % i, [128, H + NT], BF16)) for i in range(2)]
            ub_tr = [trs(1 + NTT) for _ in range(2)]
            dg = [e2.enter_context(sbuf_t(nc, "dgA%d" % i, [128, CK, 128], BF16)) for i in range(2)]
            dg_tr = trs(2)
            sgt = [e2.enter_context(sbuf_t(nc, "sgt%d" % i, [128, TT], F32)) for i in range(2)]
            sgt_tr = trs(2)
            ut = [e2.enter_context(sbuf_t(nc, "ut%d" % i, [128, TT], F32)) for i in range(3)]
            ut_tr = trs(3)
            blocks = []
            for i in range(NCc):
                blocks += [NCc + i, i]

            def load_w(bi):
                if bi < len(blocks):
                    wload(kb, wb[bi % NWB], wb_tr[bi % NWB], A["wi"][blocks[bi]], WN)
            load_w(0)
            load_w(1)
            tc = 0
            uc_ = 0
            for i in range(NCc):
                load_w(2 * i + 2)
                load_w(2 * i + 3)
                p = i % 2
                wg_, wv = (2 * i) % NWB, (2 * i + 1) % NWB
                build_diag(kb, nc, cst, dg[p], dg_tr[p], acw, acw_tr, CK, i)
                for j, (s0, w) in enumerate(cfg.tiles):
                    k = tc % 2
                    tc += 1
                    pa, patr = ps.next("mm")
                    mm_block(kb, nc, cfg, pa, patr, wb[wg_], wb_tr[wg_], hT, hT_tr, s0, w)
                    kb.ins("act", lambda: nc.scalar.activation(out=sgt[k][:, 0:w], in_=pa[:, 0:w], func=AF.Sigmoid),
                           rd=[patr], wr=[sgt_tr[k]])
                    pb_, pbtr = ps.next("mm")
                    mm_block(kb, nc, cfg, pb_, pbtr, wb[wv], wb_tr[wv], hT, hT_tr, s0, w)
                    kb.ins("dve", lambda: nc.vector.tensor_tensor(out=ub[p][:, s0:s0 + w], in0=pb_[:, 0:w],
                                                                  in1=sgt[k][:, 0:w], op=ALU.mult),
                           rd=[pbtr, sgt_tr[k]], wr=[ub_tr[p][j]])
                for jj in range(NTT):
                    s0 = H + jj * TT
                    pc, pctr = ps.next("aux")
                    for tap in range(CK):
                        o_ = s0 - (CK - 1) + tap
                        kb.ins("pe", lambda: nc.tensor.matmul(pc[:, 0:TT], lhsT=dg[p][:, tap, :], rhs=ub[p][:, o_:o_ + TT],
                                                              start=(tap == 0), stop=(tap == CK - 1)),
                               rd=[dg_tr[p], ub_tr[p][jj], ub_tr[p][jj + 1]], wr=[pctr])
                    k = uc_ % 3
                    uc_ += 1
                    kb.ins("act", lambda: nc.scalar.activation(out=ut[k][:, :], in_=pc[:, 0:TT], func=AF.Identity,
                                                               bias=acb[:, i:i + 1]),
                           rd=[pctr, acb_tr], wr=[ut_tr[k]])
                    kb.dma("sp", A["uc"][i * 128:(i + 1) * 128, jj * TT:(jj + 1) * TT], ut[k][:, :], rd=[ut_tr[k]])
            kb.barrier()
        with ExitStack() as e2:
            NWB = 3
            wb = [e2.enter_context(sbuf_t(nc, "wbq%d" % i, [128, WN], BF16)) for i in range(NWB)]
            wb_tr = trs(NWB)
            yb = [e2.enter_context(sbuf_t(nc, "yb%d" % i, [128, H + NT], BF16)) for i in range(2)]
            yb_tr = [trs(1 + NTT) for _ in range(2)]
            dg = [e2.enter_context(sbuf_t(nc, "dgq%d" % i, [128, 4, 128], BF16)) for i in range(2)]
            dg_tr = trs(2)
            qt = [e2.enter_context(sbuf_t(nc, "qt%d" % i, [128, TT], F32)) for i in range(3)]
            qt_tr = trs(3)
            nb = 4 * NH

            def load_w(bi):
                if bi < nb:
                    wload(kb, wb[bi % NWB], wb_tr[bi % NWB], A["wi"][2 * NCc + bi], WN)
            load_w(0)
            load_w(1)
            qc = 0
            for b in range(nb):
                load_w(b + 2)
                q = b % 2
                wi_ = b % NWB
                if b < 3 * NH:
                    build_diag(kb, nc, cst, dg[q], dg_tr[q], dcw, dcw_tr, 4, b)
                    for j, (s0, w) in enumerate(cfg.tiles):
                        pa, patr = ps.next("mm")
                        mm_block(kb, nc, cfg, pa, patr, wb[wi_], wb_tr[wi_], hT, hT_tr, s0, w)
                        kb.ins("act", lambda: nc.scalar.copy(out=yb[q][:, s0:s0 + w], in_=pa[:, 0:w]),
                               rd=[patr], wr=[yb_tr[q][j]])
                    for jj in range(NTT):
                        s0 = H + jj * TT
                        pc, pctr = ps.next("aux")
                        for tap in range(4):
                            o_ = s0 - 3 + tap
                            kb.ins("pe", lambda: nc.tensor.matmul(pc[:, 0:TT], lhsT=dg[q][:, tap, :], rhs=yb[q][:, o_:o_ + TT],
                                                                  start=(tap == 0), stop=(tap == 3)),
                                   rd=[dg_tr[q], yb_tr[q][jj], yb_tr[q][jj + 1]], wr=[pctr])
                        k = qc % 3
                        qc += 1
                        kb.ins("act", lambda: nc.scalar.activation(out=qt[k][:, :], in_=pc[:, 0:TT], func=AF.Silu),
                               rd=[pctr], wr=[qt_tr[k]])
                        kb.dma("sp", A["qkv"][b * 128:(b + 1) * 128, jj * TT:(jj + 1) * TT], qt[k][:, :], rd=[qt_tr[k]])
                else:
                    hh = b - 3 * NH
                    for jj in range(NTT):
                        s0 = H + jj * TT
                        pa, patr = ps.next("mm")
                        mm_block(kb, nc, cfg, pa, patr, wb[wi_], wb_tr[wi_], hT, hT_tr, s0, TT)
                        k = qc % 3
                        qc += 1
                        kb.ins("act", lambda: nc.scalar.activation(out=qt[k][:, :], in_=pa[:, 0:TT], func=AF.Silu),
                               rd=[patr], wr=[qt_tr[k]])
                        kb.dma("sp", A["zs"][hh * 128:(hh + 1) * 128, jj * TT:(jj + 1) * TT], qt[k][:, :], rd=[qt_tr[k]])
            wba = e2.enter_context(sbuf_t(nc, "wba", [128, KC, 2 * NH], BF16))
            wba_tr = TR()
            kb.dma("pool", wba[:], A["wba"], wr=[wba_tr])
            nea = e2.enter_context(sbuf_t(nc, "nea", [NH, 1], F32))
            nea_tr = TR()
            kb.ins("act", lambda: nc.scalar.activation(out=nea[:, :], in_=alog[:, :], func=AF.Exp), rd=[alog_tr], wr=[nea_tr])
            kb.ins("dve", lambda: nc.vector.tensor_scalar(out=nea[:, :], in0=nea[:, :], scalar1=-1.0, scalar2=None, op0=ALU.mult),
                   rd=[nea_tr], wr=[nea_tr])
            bt = [e2.enter_context(sbuf_t(nc, "bt%d" % i, [NH, TT], F32)) for i in range(2)]
            bt_tr = trs(2)
            gt = [e2.enter_context(sbuf_t(nc, "gt%d" % i, [NH, TT], F32)) for i in range(2)]
            gt_tr = trs(2)
            for jj in range(NTT):
                s0 = H + jj * TT
                k = jj % 2
                rdh = hT_trs_for(cfg, hT_tr, s0, TT)
                pb_, pbtr = ps.next("aux")
                for kc in range(KC):
                    kb.ins("pe", lambda: nc.tensor.matmul(pb_[0:NH, 0:TT], lhsT=wba[:, kc, 0:NH], rhs=hT[:, kc, s0:s0 + TT],
                                                          start=(kc == 0), stop=(kc == KC - 1)), rd=[wba_tr] + rdh, wr=[pbtr])
                kb.ins("act", lambda: nc.scalar.activation(out=bt[k][:, :], in_=pb_[0:NH, 0:TT], func=AF.Sigmoid),
                       rd=[pbtr], wr=[bt_tr[k]])
                kb.dma("sp", A["beta"][:, jj * TT:(jj + 1) * TT], bt[k][:, :], rd=[bt_tr[k]])
                pa, patr = ps.next("aux")
                for kc in range(KC):
                    kb.ins("pe", lambda: nc.tensor.matmul(pa[0:NH, 0:TT], lhsT=wba[:, kc, NH:2 * NH], rhs=hT[:, kc, s0:s0 + TT],
                                                          start=(kc == 0), stop=(kc == KC - 1)), rd=[wba_tr] + rdh, wr=[patr])
                kb.ins("act", lambda: nc.scalar.activation(out=gt[k][:, :], in_=pa[0:NH, 0:TT], func=AF.Exp, bias=dtb[:, 0:1]),
                       rd=[patr, dtb_tr], wr=[gt_tr[k]])
                kb.ins("act", lambda: nc.scalar.activation(out=gt[k][:, :], in_=gt[k][:, :], func=AF.Ln, bias=cst.one[0:NH, :]),
                       rd=[gt_tr[k], cst.tr], wr=[gt_tr[k]])
                kb.ins("dve", lambda: nc.vector.tensor_scalar(out=gt[k][:, :], in0=gt[k][:, :], scalar1=nea[:, 0:1], scalar2=None,
                                                              op0=ALU.mult), rd=[gt_tr[k], nea_tr], wr=[gt_tr[k]])
                kb.dma("sp", A["g"][:, jj * TT:(jj + 1) * TT], gt[k][:, :], rd=[gt_tr[k]])
            kb.barrier()


def _act_rsqrt(kb, nc, cst, out_ap, out_tr, in_ap, in_trs, scale):
    kb.ins("act", lambda: nc.scalar.activation(out=out_ap, in_=in_ap, func=AF.Ln, scale=scale, bias=cst.eps[0:out_ap.shape[0], :]),
           rd=list(in_trs) + [cst.tr], wr=[out_tr])
    kb.ins("act", lambda: nc.scalar.activation(out=out_ap, in_=out_ap, func=AF.Exp, scale=-0.5), rd=[out_tr], wr=[out_tr])


def dn_gate_prep(kb, nc, es, cfg, ps, cst, A):
    NH, NT, NCH = cfg.NH, cfg.NT, cfg.NCH
    NCOL = NCH * NH
    G = {}
    c2 = es.enter_context(sbuf_t(nc, "c2", [NH, NH * 128 + NT], F32))
    G["c2_tr"] = TR()
    kb.dma("sp", c2[:], A["cst2"], wr=[G["c2_tr"]])
    G["eh"] = c2
    grow = es.enter_context(sbuf_t(nc, "grow", [NH, NT], F32))
    brow = es.enter_context(sbuf_t(nc, "brow", [NH, NT], F32))
    gcrow = es.enter_context(sbuf_t(nc, "gcrow", [NH, NT], F32))
    egrow = es.enter_context(sbuf_t(nc, "egrow", [NH, NT], F32))
    rows_tr = TR()
    kb.dma("sp", grow[:], A["g"], wr=[rows_tr])
    kb.dma("sp", brow[:], A["beta"], wr=[rows_tr])
    kb.ins("dve", lambda: nc.vector.tensor_tensor_scan(out=gcrow[:, :], data0=c2[:, NH * 128:NH * 128 + NT], data1=grow[:, :],
                                                       initial=0.0, op0=ALU.mult, op1=ALU.add),
           rd=[rows_tr, G["c2_tr"]], wr=[rows_tr])
    kb.ins("act", lambda: nc.scalar.activation(out=egrow[:, :], in_=gcrow[:, :], func=AF.Exp), rd=[rows_tr], wr=[rows_tr])
    G.update(grow=grow, brow=brow, gcrow=gcrow, egrow=egrow, rows_tr=rows_tr)
    names = ("Gt", "Bt", "GC", "EKD", "EBG")
    for nm in names:
        G[nm] = es.enter_context(sbuf_t(nc, nm, [64, NCOL], F32))
    G["EGL"] = es.enter_context(sbuf_t(nc, "EGL", [128, NCOL], F32))
    tk = TR()
    G["tk_tr"] = tk
    pg, pgtr = ps.next("aux")
    pb_, pbtr = ps.next("aux")
    for n in range(NCH):
        kb.ins("pe", lambda: nc.tensor.transpose(pg[0:64, n * NH:(n + 1) * NH], grow[:, n * 64:(n + 1) * 64], cst.c("ident")[0:NH, 0:NH]),
               rd=[rows_tr, cst.tr], wr=[pgtr])
        kb.ins("pe", lambda: nc.tensor.transpose(pb_[0:64, n * NH:(n + 1) * NH], brow[:, n * 64:(n + 1) * 64], cst.c("ident")[0:NH, 0:NH]),
               rd=[rows_tr, cst.tr], wr=[pbtr])
    kb.ins("dve", lambda: nc.vector.tensor_copy(out=G["Gt"][:, :], in_=pg[0:64, 0:NCOL]), rd=[pgtr], wr=[tk])
    kb.ins("dve", lambda: nc.vector.tensor_copy(out=G["Bt"][:, :], in_=pb_[0:64, 0:NCOL]), rd=[pbtr], wr=[tk])
    pc, pctr = ps.next("aux")
    kb.ins("pe", lambda: nc.tensor.matmul(pc[0:64, 0:NCOL], lhsT=cst.c("tri", 64), rhs=G["Gt"][:, :], start=True, stop=True),
           rd=[tk, cst.tr], wr=[pctr])
    kb.ins("dve", lambda: nc.vector.tensor_copy(out=G["GC"][:, :], in_=pc[0:64, 0:NCOL]), rd=[pctr], wr=[tk])
    pl, pltr = ps.next("aux")
    kb.ins("pe", lambda: nc.tensor.matmul(pl[:, 0:NCOL], lhsT=cst.c("sel63", 64), rhs=G["GC"][:, :], start=True, stop=True),
           rd=[tk, cst.tr], wr=[pltr])
    kb.ins("act", lambda: nc.scalar.activation(out=G["EGL"][:, :], in_=pl[:, 0:NCOL], func=AF.Exp), rd=[pltr], wr=[tk])
    kb.ins("dve", lambda: nc.vector.tensor_tensor(out=G["EKD"][:, :], in0=pl[0:64, 0:NCOL], in1=G["GC"][:, :], op=ALU.subtract),
           rd=[pltr, tk], wr=[tk])
    kb.ins("act", lambda: nc.scalar.activation(out=G["EKD"][:, :], in_=G["EKD"][:, :], func=AF.Exp), rd=[tk], wr=[tk])
    kb.ins("act", lambda: nc.scalar.activation(out=G["EBG"][:, :], in_=G["GC"][:, :], func=AF.Exp), rd=[tk], wr=[tk])
    kb.ins("dve", lambda: nc.vector.tensor_tensor(out=G["EBG"][:, :], in0=G["EBG"][:, :], in1=G["Bt"][:, :], op=ALU.mult),
           rd=[tk], wr=[tk])
    return G


def phase_dn1(kb, nc, cfg, ps, cst, A):
    NH, NT, NCH, TD, CPD, NTD = cfg.NH, cfg.NT, cfg.NCH, cfg.TD, cfg.CPD, cfg.NTD
    ps.group("mm", [0, 1, 2, 3, 4, 5])
    ps.group("aux", [6, 7])
    with ExitStack() as es:
        G = dn_gate_prep(kb, nc, es, cfg, ps, cst, A)
        tk = G["tk_tr"]
        kb.dma("sp", A["egl"], G["EGL"][:, :], rd=[tk])
        CT = {}
        ct_tr = TR()
        for nm in ("pm1", "pm2", "ns1", "ns2", "id64"):
            CT[nm] = es.enter_context(sbuf_t(nc, "ct_" + nm, [64, TD], F32))
            for n in range(CPD):
                src = cst.c("ident")[0:64, 0:64] if nm == "id64" else cst.c(nm, 64)
                kb.ins("dve", lambda: nc.vector.tensor_copy(out=CT[nm][:, n * 64:(n + 1) * 64], in_=src), rd=[cst.tr], wr=[ct_tr])
        wT = es.enter_context(sbuf_t(nc, "wT", [128, NT], F32))
        qd = es.enter_context(sbuf_t(nc, "qd", [128, NT], F32))
        aqk = es.enter_context(sbuf_t(nc, "aqk", [64, NT], F32))
        utok = es.enter_context(sbuf_t(nc, "utok", [64, NCH, 128], F32))
        kd = es.enter_context(sbuf_t(nc, "kd", [64, NCH, 128], F32))
        wT_tr, qd_tr, aqk_tr, utok_tr, kd_tr = trs(NTD), trs(NTD), trs(NTD), trs(NTD), trs(NTD)
        NG = 2
        TB = []
        for gi in range(NG):
            t = {}
            for nm in ("q", "k", "v", "sq", "rn", "qn", "kn", "kbf"):
                t[nm] = es.enter_context(sbuf_t(nc, "t%s%d" % (nm, gi), [128, TD], F32))
            for nm in ("d1", "d2", "nd1", "nd2", "X0", "X1", "Y0", "Y1", "P0", "P1"):
                t[nm] = es.enter_context(sbuf_t(nc, "t%s%d" % (nm, gi), [64, TD], F32))
            for nm in ("kbg", "vb"):
                t[nm] = es.enter_context(sbuf_t(nc, "t%s%d" % (nm, gi), [64, CPD, 128], F32))
            for nm in list(t.keys()):
                t[nm + "_tr"] = TR()
            TB.append(t)
        S2 = [es.enter_context(sbuf_t(nc, "S2_%d" % i, [128, 256], F32)) for i in range(2)]
        S2_tr = trs(2)
        vn = [es.enter_context(sbuf_t(nc, "vn%d" % i, [64, 256], F32)) for i in range(2)]
        vna_tr, vnb_tr = trs(2), trs(2)
        ot = [es.enter_context(sbuf_t(nc, "ot%d" % i, [128, TD], F32)) for i in range(2)]
        ot_tr = trs(2)
        ident = cst.c("ident")
        eh = G["eh"]
        qscale = 128.0 ** -0.5

        def tile_prep(h, td, t):
            t0 = td * TD
            sl = slice(t0, t0 + TD)
            ehh = eh[:, h * 128:(h + 1) * 128]
            kb.dma("sp", t["q"][:, :], A["qkv"][h * 128:(h + 1) * 128, sl], wr=[t["q_tr"]])
            kb.dma("sp", t["k"][:, :], A["qkv"][(NH + h) * 128:(NH + h + 1) * 128, sl], wr=[t["k_tr"]])
            kb.dma("sp", t["v"][:, :], A["qkv"][(2 * NH + h) * 128:(2 * NH + h + 1) * 128, sl], wr=[t["v_tr"]])
            for src, dst in (("q", "qn"), ("k", "kn")):
                kb.ins("act", lambda: nc.scalar.activation(out=t["sq"][:, :], in_=t[src][:, :], func=AF.Square),
                       rd=[t[src + "_tr"]], wr=[t["sq_tr"]])
                pss, psstr = ps.next("mm")
                kb.ins("pe", lambda: nc.tensor.matmul(pss[:, 0:TD], lhsT=cst.ones_f[:], rhs=t["sq"][:, :], start=True, stop=True),
                       rd=[t["sq_tr"], cst.tr], wr=[psstr])
                _act_rsqrt(kb, nc, cst, t["rn"][:, :], t["rn_tr"], pss[:, 0:TD], [psstr], 1.0)
                if src == "q":
                    kb.ins("dve", lambda: nc.vector.scalar_tensor_tensor(out=t["qn"][:, :], in0=t["q"][:, :], scalar=qscale,
                                                                         in1=t["rn"][:, :], op0=ALU.mult, op1=ALU.mult),
                           rd=[t["q_tr"], t["rn_tr"]], wr=[t["qn_tr"]])
                else:
                    kb.ins("dve", lambda: nc.vector.tensor_tensor(out=t["kn"][:, :], in0=t["k"][:, :], in1=t["rn"][:, :], op=ALU.mult),
                           rd=[t["k_tr"], t["rn_tr"]], wr=[t["kn_tr"]])
            pbb, pbbtr = ps.next("mm")
            kb.ins("pe", lambda: nc.tensor.matmul(pbb[:, 0:TD], lhsT=ehh, rhs=G["brow"][:, sl], start=True, stop=True),
                   rd=[G["c2_tr"], G["rows_tr"]], wr=[pbbtr])
            kb.ins("dve", lambda: nc.vector.tensor_tensor(out=t["kbf"][:, :], in0=pbb[:, 0:TD], in1=t["kn"][:, :], op=ALU.mult),
                   rd=[pbbtr, t["kn_tr"]], wr=[t["kbf_tr"]])
            peg, pegtr = ps.next("mm")
            kb.ins("pe", lambda: nc.tensor.matmul(peg[:, 0:TD], lhsT=ehh, rhs=G["egrow"][:, sl], start=True, stop=True),
                   rd=[G["c2_tr"], G["rows_tr"]], wr=[pegtr])
            kb.ins("dve", lambda: nc.vector.tensor_tensor(out=qd[:, sl], in0=peg[:, 0:TD], in1=t["qn"][:, :], op=ALU.mult),
                   rd=[pegtr, t["qn_tr"]], wr=[qd_tr[td]])
            for pm, dd, sign in (("pm1", "d1", -1.0), ("pm2", "d2", 1.0)):
                pp_, pptr = ps.next("mm")
                kb.ins("pe", lambda: nc.tensor.matmul(pp_[0:64, 0:TD], lhsT=ehh[:, 0:64], rhs=G["gcrow"][:, sl], start=True, stop=False),
                       rd=[G["c2_tr"], G["rows_tr"]], wr=[pptr])
                kb.ins("pe", lambda: nc.tensor.matmul(pp_[0:64, 0:TD], lhsT=ident[0:64, 0:64], rhs=CT[pm][:, :], start=False, stop=True),
                       rd=[cst.tr, ct_tr], wr=[pptr])
                for n in range(CPD):
                    col = (td * CPD + n) * NH + h
                    if sign < 0:
                        kb.ins("dve", lambda: nc.vector.tensor_scalar(out=t[dd][:, n * 64:(n + 1) * 64], in0=pp_[0:64, n * 64:(n + 1) * 64],
                                                                      scalar1=-1.0, scalar2=G["GC"][:, col:col + 1], op0=ALU.mult, op1=ALU.add),
                               rd=[pptr, tk], wr=[t[dd + "_tr"]])
                    else:
                        kb.ins("dve", lambda: nc.vector.tensor_scalar(out=t[dd][:, n * 64:(n + 1) * 64], in0=pp_[0:64, n * 64:(n + 1) * 64],
                                                                      scalar1=G["GC"][:, col:col + 1], scalar2=None, op0=ALU.subtract),
                               rd=[pptr, tk], wr=[t[dd + "_tr"]])
                kb.ins("act", lambda: nc.scalar.activation(out=t[dd][:, :], in_=t[dd][:, :], func=AF.Exp), rd=[t[dd + "_tr"]], wr=[t[dd + "_tr"]])
            kb.ins("dve", lambda: nc.vector.tensor_tensor(out=t["nd1"][:, :], in0=t["d1"][:, :], in1=CT["ns1"][:, :], op=ALU.mult),
                   rd=[t["d1_tr"], ct_tr], wr=[t["nd1_tr"]])
            kb.ins("dve", lambda: nc.vector.tensor_tensor(out=t["nd2"][:, :], in0=t["d2"][:, :], in1=CT["ns2"][:, :], op=ALU.mult),
                   rd=[t["d2_tr"], ct_tr], wr=[t["nd2_tr"]])
            pk1, pk1tr = ps.next("mm")
            pk2, pk2tr = ps.next("mm")
            pq2, pq2tr = ps.next("mm")
            for n in range(CPD):
                ns = slice(n * 64, (n + 1) * 64)
                kb.ins("pe", lambda: nc.tensor.matmul(pk1[0:64, ns], lhsT=t["kbf"][:, ns], rhs=t["kn"][:, ns], start=True, stop=True),
                       rd=[t["kbf_tr"], t["kn_tr"]], wr=[pk1tr])
                kb.ins("pe", lambda: nc.tensor.matmul(pk2[0:64, ns], lhsT=t["kn"][:, ns], rhs=t["kbf"][:, ns], start=True, stop=True),
                       rd=[t["kbf_tr"], t["kn_tr"]], wr=[pk2tr])
                kb.ins("pe", lambda: nc.tensor.matmul(pq2[0:64, ns], lhsT=t["kn"][:, ns], rhs=t["qn"][:, ns], start=True, stop=True),
                       rd=[t["qn_tr"], t["kn_tr"]], wr=[pq2tr])
            kb.ins("dve", lambda: nc.vector.tensor_tensor(out=t["Y0"][:, :], in0=pk1[0:64, 0:TD], in1=t["nd1"][:, :], op=ALU.mult),
                   rd=[pk1tr, t["nd1_tr"]], wr=[t["Y0_tr"]])
            kb.ins("dve", lambda: nc.vector.tensor_tensor(out=t["X0"][:, :], in0=pk2[0:64, 0:TD], in1=t["nd2"][:, :], op=ALU.mult),
                   rd=[pk2tr, t["nd2_tr"]], wr=[t["X0_tr"]])
            kb.ins("dve", lambda: nc.vector.tensor_tensor(out=aqk[:, sl], in0=pq2[0:64, 0:TD], in1=t["d2"][:, :], op=ALU.mult),
                   rd=[pq2tr, t["d2_tr"]], wr=[aqk_tr[td]])
            kb.ins("dve", lambda: nc.vector.tensor_tensor(out=t["P0"][:, :], in0=t["X0"][:, :], in1=CT["id64"][:, :], op=ALU.add),
                   rd=[t["X0_tr"], ct_tr], wr=[t["P0_tr"]])

        def tile_level(t, lev):
            a, b = str(lev % 2), str((lev + 1) % 2)
            X, Y, P, Xn, Yn, Pn = "X" + a, "Y" + a, "P" + a, "X" + b, "Y" + b, "P" + b
            last = lev == 4
            if not last:
                px, pxtr = ps.next("mm")
                for n in range(CPD):
                    ns = slice(n * 64, (n + 1) * 64)
                    kb.ins("pe", lambda: nc.tensor.matmul(px[0:64, ns], lhsT=t[Y][:, ns], rhs=t[X][:, ns], start=True, stop=True),
                           rd=[t[X + "_tr"], t[Y + "_tr"]], wr=[pxtr])
            py, pytr = ps.next("mm")
            for n in range(CPD):
                ns = slice(n * 64, (n + 1) * 64)
                kb.ins("pe", lambda: nc.tensor.matmul(py[0:64, ns], lhsT=t[X][:, ns], rhs=t[Y][:, ns], start=True, stop=True),
                       rd=[t[X + "_tr"], t[Y + "_tr"]], wr=[pytr])
            if not last:
                kb.ins("act", lambda: nc.scalar.copy(out=t[Xn][:, :], in_=px[0:64, 0:TD]), rd=[pxtr], wr=[t[Xn + "_tr"]])
            kb.ins("dve", lambda: nc.vector.tensor_copy(out=t[Yn][:, :], in_=py[0:64, 0:TD]), rd=[pytr], wr=[t[Yn + "_tr"]])
            return (t, P, Yn, Pn)

        def tile_level_b(t, P, Yn, Pn):
            pp_, pptr = ps.next("mm")
            for n in range(CPD):
                ns = slice(n * 64, (n + 1) * 64)
                kb.ins("pe", lambda: nc.tensor.matmul(pp_[0:64, ns], lhsT=t[Yn][:, ns], rhs=t[P][:, ns], start=True, stop=True),
                       rd=[t[Yn + "_tr"], t[P + "_tr"]], wr=[pptr])
            kb.ins("dve", lambda: nc.vector.tensor_tensor(out=t[Pn][:, :], in0=pp_[0:64, 0:TD], in1=t[P][:, :], op=ALU.add),
                   rd=[pptr, t[P + "_tr"]], wr=[t[Pn + "_tr"]])

        def tile_post(h, td, t):
            P = "P1"
            ptk, ptktr = ps.next("mm")
            ptv, ptvtr = ps.next("mm")
            for n in range(CPD):
                ns = slice(n * 64, (n + 1) * 64)
                kb.ins("pe", lambda: nc.tensor.transpose(ptk[0:64, n * 128:(n + 1) * 128], t["kn"][:, ns], ident),
                       rd=[t["kn_tr"], cst.tr], wr=[ptktr])
                kb.ins("pe", lambda: nc.tensor.transpose(ptv[0:64, n * 128:(n + 1) * 128], t["v"][:, ns], ident),
                       rd=[t["v_tr"], cst.tr], wr=[ptvtr])
            if DN_DEBUG == 5:
                return
            for n in range(CPD):
                ng = td * CPD + n
                col = ng * NH + h
                ks = slice(n * 128, (n + 1) * 128)
                if DN_DEBUG in (0, 6, 7, 61):
                    kb.ins("dve", lambda: nc.vector.tensor_scalar(out=t["kbg"][:, n, :], in0=ptk[0:64, ks], scalar1=G["EBG"][:, col:col + 1],
                                                                  scalar2=None, op0=ALU.mult), rd=[ptktr, tk], wr=[t["kbg_tr"]])
                if DN_DEBUG == 61:
                    continue
                kb.ins("dve", lambda: nc.vector.tensor_scalar(out=kd[:, ng, :], in0=ptk[0:64, ks], scalar1=G["EKD"][:, col:col + 1], scalar2=None,
                                                              op0=ALU.mult), rd=[ptktr, tk], wr=[kd_tr[td]])
                if DN_DEBUG == 62:
                    continue
                kb.ins("dve", lambda: nc.vector.tensor_scalar(out=t["vb"][:, n, :], in0=ptv[0:64, ks], scalar1=G["Bt"][:, col:col + 1],
                                                              scalar2=None, op0=ALU.mult), rd=[ptvtr, tk], wr=[t["vb_tr"]])
            if DN_DEBUG in (6, 61, 62):
                return
            pu, putr = ps.next("mm")
            pw, pwtr = ps.next("mm")
            for n in range(CPD):
                ns = slice(n * 64, (n + 1) * 64)
                kb.ins("pe", lambda: nc.tensor.matmul(pu[0:64, n * 128:(n + 1) * 128], lhsT=t[P][:, ns], rhs=t["vb"][:, n, :], start=True, stop=True),
                       rd=[t[P + "_tr"], t["vb_tr"]], wr=[putr])
                kb.ins("pe", lambda: nc.tensor.matmul(pw[:, ns], lhsT=t["kbg"][:, n, :], rhs=t[P][:, ns], start=True, stop=True),
                       rd=[t[P + "_tr"], t["kbg_tr"]], wr=[pwtr])
            if DN_DEBUG == 7:
                return
            kb.ins("act", lambda: nc.scalar.copy(out=utok[:, td * CPD:(td + 1) * CPD, :].rearrange("p a b -> p (a b)") if False else utok[:, td * CPD:(td + 1) * CPD, :],
                                                 in_=pu[0:64, 0:CPD * 128].rearrange("p (a b) -> p a b", b=128)),
                   rd=[putr], wr=[utok_tr[td]])
            kb.ins("dve", lambda: nc.vector.tensor_copy(out=wT[:, td * TD:(td + 1) * TD], in_=pw[:, 0:TD]), rd=[pwtr], wr=[wT_tr[td]])

        for h in range(NH if DN_DEBUG != 1 else 0):
            for g0 in range(0, NTD, NG):
                tds = list(range(g0, min(NTD, g0 + NG)))
                for gi, td in enumerate(tds):
                    tile_prep(h, td, TB[gi])
                if DN_DEBUG == 2:
                    continue
                for lev in range(5):
                    pend = [tile_level(TB[gi], lev) for gi, td in enumerate(tds)]
                    for pnd in pend:
                        tile_level_b(*pnd)
                if DN_DEBUG == 3:
                    continue
                for gi, td in enumerate(tds):
                    tile_post(h, td, TB[gi])
            if DN_DEBUG in (2, 3, 4, 5, 6, 7, 61, 62):
                continue
            kb.dma("sp", A["wT_s"][h * 128:(h + 1) * 128, :], wT[:, :], rd=wT_tr)
            kb.dma("sp", A["qd_s"][h * 128:(h + 1) * 128, :], qd[:, :], rd=qd_tr)
            kb.dma("sp", A["aqk_s"][h * 64:(h + 1) * 64, :], aqk[:, :], rd=aqk_tr)
            kb.dma("sp", A["kd_s"][h * 64:(h + 1) * 64, :], kd[:, :, :].rearrange("p a b -> p (a b)") if False else kd[:, :, :], rd=kd_tr)
            if DN_DEBUG == 8:
                continue
            kb.ins("dve", lambda: nc.vector.memset(S2[0][:, 0:128], 0.0), wr=[S2_tr[0]])
            kb.ins("dve", lambda: nc.vector.tensor_copy(out=S2[0][:, 128:256], in_=ident), rd=[cst.tr], wr=[S2_tr[0]])
            po, potr = None, None
            for ng in range(NCH):
                td, nl = ng // CPD, ng % CPD
                cs = slice(ng * 64, (ng + 1) * 64)
                col = ng * NH + h
                c_, n_ = ng % 2, (ng + 1) % 2
                pws, pwstr = ps.next("mm")
                kb.ins("pe", lambda: nc.tensor.matmul(pws[0:64, 0:256], lhsT=wT[:, cs], rhs=S2[c_][:, :], start=True, stop=True),
                       rd=[wT_tr[td], S2_tr[c_]], wr=[pwstr])
                if nl == 0:
                    po, potr = ps.next("aux")
                kb.ins("pe", lambda: nc.tensor.matmul(po[:, nl * 64:(nl + 1) * 64], lhsT=S2[c_][:, 0:128], rhs=qd[:, cs], start=True, stop=False),
                       rd=[S2_tr[c_], qd_tr[td]], wr=[potr])
                kb.ins("dve", lambda: nc.vector.tensor_tensor(out=vn[c_][:, 0:128], in0=utok[:, ng, :], in1=pws[0:64, 0:128], op=ALU.subtract),
                       rd=[utok_tr[td], pwstr], wr=[vna_tr[c_]])
                kb.ins("dve", lambda: nc.vector.tensor_scalar(out=vn[c_][:, 128:256], in0=pws[0:64, 128:256], scalar1=-1.0, scalar2=None,
                                                              op0=ALU.mult), rd=[pwstr], wr=[vnb_tr[c_]])
                kb.ins("pe", lambda: nc.tensor.matmul(po[:, nl * 64:(nl + 1) * 64], lhsT=vn[c_][:, 0:128], rhs=aqk[:, cs], start=False, stop=True),
                       rd=[vna_tr[c_], aqk_tr[td]], wr=[potr])
                pst, psttr = ps.next("mm")
                kb.ins("pe", lambda: nc.tensor.matmul(pst[:, 0:256], lhsT=kd[:, ng, :], rhs=vn[c_][:, :], start=True, stop=True),
                       rd=[kd_tr[td], vna_tr[c_], vnb_tr[c_]], wr=[psttr])
                kb.ins("dve", lambda: nc.vector.scalar_tensor_tensor(out=S2[n_][:, :], in0=S2[c_][:, :], scalar=G["EGL"][:, col:col + 1],
                                                                     in1=pst[:, 0:256], op0=ALU.mult, op1=ALU.add),
                       rd=[S2_tr[c_], psttr, tk], wr=[S2_tr[n_]])
                if nl == CPD - 1:
                    o_ = td % 2
                    kb.ins("act", lambda: nc.scalar.copy(out=ot[o_][:, :], in_=po[:, 0:TD]), rd=[potr], wr=[ot_tr[o_]])
                    kb.dma("sp", A["ol"][h * 128:(h + 1) * 128, td * TD:(td + 1) * TD], ot[o_][:, :], rd=[ot_tr[o_]])
            kb.dma("sp", A["sx"][h], S2[NCH % 2][:, :], rd=[S2_tr[NCH % 2]])
        kb.barrier()
    ps.group("mm", [0, 1, 2, 3, 4])
    ps.group("aux", [5, 6, 7])


def phase_dn2(kb, nc, cfg, ps, cst, A):
    NH, NT, NCH, TD, CPD, NTD, NC = cfg.NH, cfg.NT, cfg.NCH, cfg.TD, cfg.CPD, cfg.NTD, cfg.NC
    NCOL = NCH * NH
    ps.group("mm", [0, 1, 2, 3, 4, 5])
    ps.group("aux", [6, 7])
    with ExitStack() as es:
        EGL, egl_tr = load_small(kb, nc, es, "EGL2", [128, NCOL], A["egl"])
        sel, sel_tr = load_small(kb, nc, es, "sel", [128, NC], A["sel"])
        wT = [es.enter_context(sbuf_t(nc, "wT2_%d" % i, [128, NT], F32)) for i in range(2)]
        qd = [es.enter_context(sbuf_t(nc, "qd2_%d" % i, [128, NT], F32)) for i in range(2)]
        aqk = [es.enter_context(sbuf_t(nc, "aqk2_%d" % i, [64, NT], F32)) for i in range(2)]
        kd = [es.enter_context(sbuf_t(nc, "kd2_%d" % i, [64, NCH, 128], F32)) for i in range(2)]
        op_tr = [trs(4) for _ in range(2)]
        sxj = [es.enter_context(sbuf_t(nc, "sxj%d" % i, [128, 256], F32)) for i in range(2)]
        sxj_tr = trs(2)
        PTs = [es.enter_context(sbuf_t(nc, "PTs%d" % i, [128, 128], F32)) for i in range(2)]
        PTs_tr = trs(2)
        Rb = [es.enter_context(sbuf_t(nc, "Rb%d" % i, [128, 128], F32)) for i in range(2)]
        Rb_tr = trs(2)
        acc = es.enter_context(sbuf_t(nc, "acc", [128, 128], F32))
        acc_tr = TR()
        S = [es.enter_context(sbuf_t(nc, "S_%d" % i, [128, 128], F32)) for i in range(2)]
        S_tr = trs(2)
        vn = [es.enter_context(sbuf_t(nc, "vn2_%d" % i, [64, 128], F32)) for i in range(2)]
        vn_tr = trs(2)
        ot = [es.enter_context(sbuf_t(nc, "ot2_%d" % i, [128, TD], F32)) for i in range(2)]
        ot_tr = trs(2)
        olt = [es.enter_context(sbuf_t(nc, "olt%d" % i, [128, TD], F32)) for i in range(2)]
        olt_tr = trs(2)
        ident = cst.c("ident")

        def load_ops(h):
            b = h % 2
            kb.dma("sp", wT[b][:, :], A["wT_s"][h * 128:(h + 1) * 128, :], wr=[op_tr[b][0]])
            kb.dma("sp", qd[b][:, :], A["qd_s"][h * 128:(h + 1) * 128, :], wr=[op_tr[b][1]])
            kb.dma("sp", aqk[b][:, :], A["aqk_s"][h * 64:(h + 1) * 64, :], wr=[op_tr[b][2]])
            kb.dma("sp", kd[b][:, :, :], A["kd_s"][h * 64:(h + 1) * 64, :, :], wr=[op_tr[b][3]])
        load_ops(0)
        jc = 0
        for h in range(NH):
            if h + 1 < NH:
                load_ops(h + 1)
            b = h % 2
            kb.ins("dve", lambda: nc.vector.memset(acc[:, :], 0.0), wr=[acc_tr])
            rcur = None
            for j in range(NC - 1):
                sb_ = jc % 2
                jc += 1
                kb.dma("sp", sxj[sb_][:, :], A["sxall"][j, h], wr=[sxj_tr[sb_]])
                rn_ = (j + 1) % 2
                if j == 0:
                    kb.ins("dve", lambda: nc.vector.tensor_copy(out=Rb[rn_][:, :], in_=sxj[sb_][:, 0:128]), rd=[sxj_tr[sb_]], wr=[Rb_tr[rn_]])
                else:
                    pt_, pttr = ps.next("mm")
                    kb.ins("pe", lambda: nc.tensor.transpose(pt_[:, 0:128], sxj[sb_][:, 128:256], ident), rd=[sxj_tr[sb_], cst.tr], wr=[pttr])
                    kb.ins("act", lambda: nc.scalar.copy(out=PTs[sb_][:, :], in_=pt_[:, 0:128]), rd=[pttr], wr=[PTs_tr[sb_]])
                    pr, prtr = ps.next("mm")
                    kb.ins("pe", lambda: nc.tensor.matmul(pr[:, 0:128], lhsT=PTs[sb_][:, :], rhs=Rb[rcur][:, :], start=True, stop=True),
                           rd=[PTs_tr[sb_], Rb_tr[rcur]], wr=[prtr])
                    kb.ins("dve", lambda: nc.vector.tensor_tensor(out=Rb[rn_][:, :], in0=pr[:, 0:128], in1=sxj[sb_][:, 0:128], op=ALU.add),
                           rd=[prtr, sxj_tr[sb_]], wr=[Rb_tr[rn_]])
                rcur = rn_
                kb.ins("dve", lambda: nc.vector.scalar_tensor_tensor(out=acc[:, :], in0=Rb[rcur][:, :], scalar=sel[:, j + 1:j + 2],
                                                                     in1=acc[:, :], op0=ALU.mult, op1=ALU.add),
                       rd=[Rb_tr[rcur], sel_tr, acc_tr], wr=[acc_tr])
            kb.ins("dve", lambda: nc.vector.tensor_copy(out=S[0][:, :], in_=acc[:, :]), rd=[acc_tr], wr=[S_tr[0]])
            po, potr = None, None
            for ng in range(NCH):
                td, nl = ng // CPD, ng % CPD
                cs = slice(ng * 64, (ng + 1) * 64)
                col = ng * NH + h
                c_, n_ = ng % 2, (ng + 1) % 2
                pws, pwstr = ps.next("mm")
                kb.ins("pe", lambda: nc.tensor.matmul(pws[0:64, 0:128], lhsT=wT[b][:, cs], rhs=S[c_][:, :], start=True, stop=True),
                       rd=[op_tr[b][0], S_tr[c_]], wr=[pwstr])
                if nl == 0:
                    po, potr = ps.next("aux")
                    o_ = td % 2
                    kb.dma("sp", olt[o_][:, :], A["ol"][h * 128:(h + 1) * 128, td * TD:(td + 1) * TD], wr=[olt_tr[o_]])
                kb.ins("pe", lambda: nc.tensor.matmul(po[:, nl * 64:(nl + 1) * 64], lhsT=S[c_][:, :], rhs=qd[b][:, cs], start=True, stop=False),
                       rd=[S_tr[c_], op_tr[b][1]], wr=[potr])
                kb.ins("dve", lambda: nc.vector.tensor_scalar(out=vn[c_][:, :], in0=pws[0:64, 0:128], scalar1=-1.0, scalar2=None, op0=ALU.mult),
                       rd=[pwstr], wr=[vn_tr[c_]])
                kb.ins("pe", lambda: nc.tensor.matmul(po[:, nl * 64:(nl + 1) * 64], lhsT=vn[c_][:, :], rhs=aqk[b][:, cs], start=False, stop=True),
                       rd=[vn_tr[c_], op_tr[b][2]], wr=[potr])
                pst, psttr = ps.next("mm")
                kb.ins("pe", lambda: nc.tensor.matmul(pst[:, 0:128], lhsT=kd[b][:, ng, :], rhs=vn[c_][:, :], start=True, stop=True),
                       rd=[op_tr[b][3], vn_tr[c_]], wr=[psttr])
                kb.ins("dve", lambda: nc.vector.scalar_tensor_tensor(out=S[n_][:, :], in0=S[c_][:, :], scalar=EGL[:, col:col + 1],
                                                                     in1=pst[:, 0:128], op0=ALU.mult, op1=ALU.add),
                       rd=[S_tr[c_], psttr, egl_tr], wr=[S_tr[n_]])
                if nl == CPD - 1:
                    o_ = td % 2
                    kb.ins("dve", lambda: nc.vector.tensor_tensor(out=ot[o_][:, :], in0=po[:, 0:TD], in1=olt[o_][:, :], op=ALU.add),
                           rd=[potr, olt_tr[o_]], wr=[ot_tr[o_]])
                    kb.dma("sp", A["o_s"][h * 128:(h + 1) * 128, td * TD:(td + 1) * TD], ot[o_][:, :], rd=[ot_tr[o_]])
        kb.barrier()
    ps.group("mm", [0, 1, 2, 3, 4])
    ps.group("aux", [5, 6, 7])


def phase_even_out(kb, nc, cfg, ps, cst, A):
    KC, NT, TT, NTT, NH = cfg.KC, cfg.NT, cfg.TT, cfg.NTT, cfg.NH
    NCc = cfg.CW // 128
    CWf = float(cfg.CW)
    with ExitStack() as es:
        hc = es.enter_context(sbuf_t(nc, "hc", [128, KC, NT], BF16))
        hc_tr = [[TR(), TR()] for _ in range(NTT)]
        lng, lng_tr = load_small(kb, nc, es, "lng", [128, NCc], A["lng"])
        lnb, lnb_tr = load_small(kb, nc, es, "lnb", [128, NCc], A["lnb"])
        gn, gn_tr = load_small(kb, nc, es, "gn", [128, 1], A["gn"])
        with ExitStack() as e2:
            us = [e2.enter_context(sbuf_t(nc, "us%d" % i, [128, NCc, 128], F32)) for i in range(2)]
            us_tr = trs(2)
            sq = [e2.enter_context(sbuf_t(nc, "sqo%d" % i, [128, NCc, 128], F32)) for i in range(2)]
            sq_tr = trs(2)
            zt = [e2.enter_context(sbuf_t(nc, "zt%d" % i, [128, NH, 128], F32)) for i in range(2)]
            zt_tr = trs(2)
            rn = e2.enter_context(sbuf_t(nc, "rno", [128, NH, 128], F32))
            rn_tr = TR()
            sm = {}
            for nm in ("mean", "msq", "var", "nmr"):
                sm[nm] = e2.enter_context(sbuf_t(nc, "sm_" + nm, [128, 128], F32))
                sm[nm + "_tr"] = TR()
            yb = [e2.enter_context(sbuf_t(nc, "yo%d" % i, [128, 128], F32)) for i in range(2)]
            yb_tr = trs(2)
            k = 0
            yc = 0
            for st_ in range(NT // 128):
                sl = slice(st_ * 128, (st_ + 1) * 128)
                jt = (st_ * 128) // TT
                b = k % 2
                k += 1
                kb.dma("sp", us[b][:, :, :], A["uc"][:, sl].rearrange("(c p) t -> p c t", p=128), wr=[us_tr[b]])
                kb.ins("act", lambda: nc.scalar.activation(out=sq[b][:, :, :], in_=us[b][:, :, :], func=AF.Square), rd=[us_tr[b]], wr=[sq_tr[b]])
                p1, p1tr = ps.next("aux")
                p2, p2tr = ps.next("aux")
                for c in range(NCc):
                    kb.ins("pe", lambda: nc.tensor.matmul(p1[:, 0:128], lhsT=cst.ones_f[:], rhs=us[b][:, c, :], start=(c == 0), stop=(c == NCc - 1)),
                           rd=[us_tr[b], cst.tr], wr=[p1tr])
                for c in range(NCc):
                    kb.ins("pe", lambda: nc.tensor.matmul(p2[:, 0:128], lhsT=cst.ones_f[:], rhs=sq[b][:, c, :], start=(c == 0), stop=(c == NCc - 1)),
                           rd=[sq_tr[b], cst.tr], wr=[p2tr])
                kb.ins("dve", lambda: nc.vector.tensor_scalar(out=sm["mean"][:, :], in0=p1[:, 0:128], scalar1=1.0 / CWf, scalar2=None, op0=ALU.mult),
                       rd=[p1tr], wr=[sm["mean_tr"]])
                kb.ins("dve", lambda: nc.vector.tensor_tensor(out=sm["msq"][:, :], in0=sm["mean"][:, :], in1=sm["mean"][:, :], op=ALU.mult),
                       rd=[sm["mean_tr"]], wr=[sm["msq_tr"]])
                kb.ins("dve", lambda: nc.vector.scalar_tensor_tensor(out=sm["var"][:, :], in0=p2[:, 0:128], scalar=1.0 / CWf, in1=sm["msq"][:, :],
                                                                     op0=ALU.mult, op1=ALU.subtract),
                       rd=[p2tr, sm["msq_tr"]], wr=[sm["var_tr"]])
                _act_rsqrt(kb, nc, cst, sm["var"][:, :], sm["var_tr"], sm["var"][:, :], [sm["var_tr"]], 1.0)
                kb.ins("dve", lambda: nc.vector.scalar_tensor_tensor(out=sm["nmr"][:, :], in0=sm["mean"][:, :], scalar=-1.0, in1=sm["var"][:, :],
                                                                     op0=ALU.mult, op1=ALU.mult),
                       rd=[sm["mean_tr"], sm["var_tr"]], wr=[sm["nmr_tr"]])
                for c in range(NCc):
                    y_ = yc % 2
                    yc += 1
                    kb.ins("dve", lambda: nc.vector.tensor_tensor(out=yb[y_][:, :], in0=us[b][:, c, :], in1=sm["var"][:, :], op=ALU.mult),
                           rd=[us_tr[b], sm["var_tr"]], wr=[yb_tr[y_]])
                    kb.ins("dve", lambda: nc.vector.tensor_tensor(out=yb[y_][:, :], in0=yb[y_][:, :], in1=sm["nmr"][:, :], op=ALU.add),
                           rd=[yb_tr[y_], sm["nmr_tr"]], wr=[yb_tr[y_]])
                    kb.ins("act", lambda: nc.scalar.activation(out=hc[:, c, sl], in_=yb[y_][:, :], func=AF.Silu,
                                                               scale=lng[:, c:c + 1], bias=lnb[:, c:c + 1]),
                           rd=[yb_tr[y_], lng_tr, lnb_tr], wr=[hc_tr[jt][0]])
                b = k % 2
                k += 1
                kb.dma("sp", us[b][:, 0:NH, :], A["o_s"][:, sl].rearrange("(c p) t -> p c t", p=128), wr=[us_tr[b]])
                kb.dma("sp", zt[b][:, :, :], A["zs"][:, sl].rearrange("(c p) t -> p c t", p=128), wr=[zt_tr[b]])
                kb.ins("act", lambda: nc.scalar.activation(out=sq[b][:, 0:NH, :], in_=us[b][:, 0:NH, :], func=AF.Square), rd=[us_tr[b]], wr=[sq_tr[b]])
                for h0 in range(0, NH, 4):
                    pss, psstr = ps.next("aux")
                    for hh in range(4):
                        kb.ins("pe", lambda: nc.tensor.matmul(pss[:, hh * 128:(hh + 1) * 128], lhsT=cst.ones_f[:], rhs=sq[b][:, h0 + hh, :],
                                                              start=True, stop=True), rd=[sq_tr[b], cst.tr], wr=[psstr])
                    _act_rsqrt(kb, nc, cst, rn[:, h0:h0 + 4, :], rn_tr, pss[:, 0:512].rearrange("p (a b) -> p a b", b=128), [psstr], 1.0 / 128.0)
                kb.ins("dve", lambda: nc.vector.scalar_tensor_tensor(out=us[b][:, 0:NH, :], in0=us[b][:, 0:NH, :], scalar=gn[:, 0:1],
                                                                     in1=rn[:, :, :], op0=ALU.mult, op1=ALU.mult),
                       rd=[us_tr[b], gn_tr, rn_tr], wr=[us_tr[b]])
                kb.ins("dve", lambda: nc.vector.tensor_tensor(out=hc[:, NCc:KC, sl], in0=us[b][:, 0:NH, :], in1=zt[b][:, :, :], op=ALU.mult),
                       rd=[us_tr[b], zt_tr[b]], wr=[hc_tr[jt][1]])
            kb.barrier()
        dense_main(kb, nc, es, cfg, ps, hc, hc_tr, A["wo"], A)
        kb.barrier()


BUILD_ONLY = False


def run_launch(cfg, emit, ext_in, ext_out, in_maps, internal=()):
    nc = bass.Bass("TRN2", target_bir_lowering=False)
    A = {}
    for name, (shape, dt) in ext_in.items():
        A[name] = nc.dram_tensor(name, list(shape), dt, kind="ExternalInput").ap()
    for name, (shape, dt) in ext_out.items():
        A[name] = nc.dram_tensor(name, list(shape), dt, kind="ExternalOutput").ap()
    for name, (shape, dt) in dict(internal).items():
        A[name] = nc.dram_tensor(name, list(shape), dt, kind="Internal").ap()
    with ExitStack() as es:
        kb = KB(nc, es)
        ps = PS(nc, es)
        cst = Const(kb, nc, es, cfg, A["cst"])
        emit(kb, nc, cfg, ps, cst, A)
        kb.barrier()
    if BUILD_ONLY:
        print("built: n_ins=%d n_wait=%d" % (kb.n_ins, kb.n_wait), flush=True)
        return None
    res = run_bass_kernel_spmd(nc, in_maps, core_ids=list(range(cfg.NC)))
    return res.results


def blk_in(W):
    Din, C = W.shape
    KC, nb = Din // 128, C // 128
    return np.ascontiguousarray(W.reshape(KC, 128, nb, 128).transpose(2, 1, 0, 3)).reshape(nb, 128, KC * 128)


def vec_pc(v):
    return np.ascontiguousarray(v.reshape(-1, 128).T)


def taps_pc(w):
    K, C = w.shape
    return np.ascontiguousarray(w.reshape(K, C // 128, 128).transpose(2, 0, 1))


def halos(xT_list, H):
    out = []
    for c in range(len(xT_list)):
        if c == 0:
            out.append(np.zeros((xT_list[0].shape[0], H), xT_list[0].dtype))
        else:
            out.append(np.ascontiguousarray(xT_list[c - 1][:, -H:]))
    return out


def _specs(d):
    return {k: (tuple(v.shape), BF16 if str(v.dtype) == "bfloat16" else F32) for k, v in d.items()}


def kernel(**inp):
    cfg = Cfg()
    NC, NT, D, H, KC, NH, NCH, FC, FF = cfg.NC, cfg.NT, cfg.D, cfg.H, cfg.KC, cfg.NH, cfg.NCH, cfg.FC, cfg.FF
    NCc = cfg.CW // 128
    CPG = cfg.PG // 128
    inp = {k: np.asarray(v, dtype=np.float32) for k, v in inp.items()}
    x = inp["x"][0]
    xT = [np.ascontiguousarray(x[c * NT:(c + 1) * NT].T) for c in range(NC)]
    cst = const_array(cfg)
    c2 = const2_array(cfg)
    XS = ((D, NT), F32)

    def launch(emit, shared, percore, ext_out, internal):
        in_maps = []
        for c in range(NC):
            m = dict(shared)
            for k, v in percore.items():
                m[k] = v[c]
            in_maps.append(m)
        ext_in = _specs(in_maps[0])
        return run_launch(cfg, emit, ext_in, ext_out, in_maps, internal)

    for l in range(cfg.depth):
        j = l // 2
        xh = halos(xT, H)
        if l % 2 == 0:
            W = inp["ev_w_in"][j]
            C0 = (2 * NCc + 4 * NH) * 128
            shared = {"cst": cst, "cst2": c2, "mng": vec_pc(inp["mix_norm_g"][l]), "acw": taps_pc(inp["a_conv_w"][j]),
                      "acb": vec_pc(inp["a_conv_b"][j]), "dcw": taps_pc(inp["dn_conv_w"][j]),
                      "alog": inp["dn_a_log"][j].reshape(NH, 1).copy(), "dtb": inp["dn_dt_bias"][j].reshape(NH, 1).copy(),
                      "wi": blk_in(W[:, :C0]),
                      "wba": np.ascontiguousarray(W[:, C0:].reshape(KC, 128, 2 * NH).transpose(1, 0, 2))}

            def emit_a(kb, nc, cfg, ps, cst_, A):
                phase_even_in(kb, nc, cfg, ps, cst_, A)
                phase_dn1(kb, nc, cfg, ps, cst_, A)
            outs_a = {"uc": ((cfg.CW, NT), F32), "zs": ((cfg.DW, NT), F32), "ol": ((cfg.DW, NT), F32), "sx": ((NH, 128, 256), F32),
                      "wT_s": ((NH * 128, NT), F32), "qd_s": ((NH * 128, NT), F32), "aqk_s": ((NH * 64, NT), F32),
                      "kd_s": ((NH * 64, NCH, 128), F32), "egl": ((128, NCH * NH), F32)}
            int_a = {"qkv": ((3 * cfg.DW, NT), F32), "beta": ((NH, NT), F32), "g": ((NH, NT), F32)}
            ra = launch(emit_a, shared, {"xT": xT, "xh": xh}, outs_a, int_a)
            del shared
            if ra is None:
                ra = [{k: np.zeros(v[0], np.float32) for k, v in outs_a.items()} for _ in range(NC)]
            sxall = np.stack([r["sx"] for r in ra], axis=0)
            sels = []
            for c in range(NC):
                s_ = np.zeros((128, NC), np.float32)
                s_[:, c] = 1.0
                sels.append(s_)
            shared = {"cst": cst, "sxall": sxall, "wo": blk_in(inp["ev_w_out"][j]), "lng": vec_pc(inp["a_ln_g"][j]),
                      "lnb": vec_pc(inp["a_ln_b"][j]), "gn": inp["dn_norm_g"][j].reshape(128, 1).copy()}
            percore = {"xT": xT, "sel": sels}
            for nm in ("uc", "zs", "ol", "wT_s", "qd_s", "aqk_s", "kd_s", "egl"):
                percore[nm] = [r[nm] for r in ra]

            def emit_b(kb, nc, cfg, ps, cst_, A):
                phase_dn2(kb, nc, cfg, ps, cst_, A)
                phase_even_out(kb, nc, cfg, ps, cst_, A)
            rb = launch(emit_b, shared, percore, {"xo": XS}, {"o_s": ((cfg.DW, NT), F32)})
            del shared, percore, ra
            if rb is not None:
                xT = [r["xo"] for r in rb]
        else:
            pcs = []
            for c in range(NC):
                pc_ = np.ones((128, 4, 16), np.float32)
                for g_, win in enumerate((2, 4, 8, 16)):
                    for t in range(16):
                        pc_[:, g_, t] = win / min(c * NT + t + 1, win)
                pcs.append(pc_)
            shared = {"cst": cst, "mng": vec_pc(inp["mix_norm_g"][l]), "scw": taps_pc(inp["sc_conv_w"][j]),
                      "psc": vec_pc(inp["pool_scale"][j]), "wi": blk_in(inp["od_w_in"][j]), "wo": blk_in(inp["od_w_out"][j]),
                      "pw": np.ascontiguousarray(inp["pool_w"][j].reshape(4, CPG, 128, cfg.PG).transpose(0, 2, 1, 3)).reshape(4, 128, CPG * cfg.PG)}

            def emit_o(kb, nc, cfg, ps, cst_, A):
                phase_odd_in(kb, nc, cfg, ps, cst_, A)
                phase_odd_out(kb, nc, cfg, ps, cst_, A)
            ro = launch(emit_o, shared, {"xT": xT, "xh": xh, "pcorr": pcs}, {"xo": XS}, {"hc": ((D, NT), BF16)})
            del shared
            if ro is not None:
                xT = [r["xo"] for r in ro]
        xh = halos(xT, H)
        lastl = l == cfg.depth - 1
        shared = {"cst": cst, "fng": vec_pc(inp["ffn_norm_g"][l]), "fcw": taps_pc(inp["ffn_conv_w"][l]),
                  "wg": blk_in(inp["ffn_w_gate"][l]), "wu": blk_in(inp["ffn_w_up"][l]), "wd": blk_in(inp["ffn_w_down"][l])}
        if lastl:
            shared["fing"] = vec_pc(inp["final_norm_g"])

        def emit_f(kb, nc, cfg, ps, cst_, A):
            phase_ffn1(kb, nc, cfg, ps, cst_, A)
            phase_ffn2(kb, nc, cfg, ps, cst_, A)
            if lastl:
                A2 = dict(A)
                A2["xT"] = A["xo"]
                phase_final(kb, nc, cfg, ps, cst_, A2)
        if lastl:
            rf = launch(emit_f, shared, {"xT": xT, "xh": xh}, {"out": XS}, {"act": ((FF, NT), BF16), "xo": XS})
            key = "out"
        else:
            rf = launch(emit_f, shared, {"xT": xT, "xh": xh}, {"xo": XS}, {"act": ((FF, NT), BF16)})
            key = "xo"
        del shared
        if rf is not None:
            xT = [r[key] for r in rf]
    out = np.concatenate([np.ascontiguousarray(t.T) for t in xT], axis=0)
    return out[None].astype(np.float32)
```
